# Optimizing a Trainium2 kernel written in Bass

```python
import jax, jax.numpy as jnp
from jax import lax
import numpy as np


D_MODEL = 1024
BATCH = 16
SEQ = 4096
DEPTH = 1

CHUNK = 64
Q_BLOCK = 128
GLA_HEADS = 4
GLA_DK = 64
GLA_DV = 128
GLA_LOWRANK = 16
GLA_TAU = 16.0
FOX_HEADS = 8
FOX_HD = 64
D_FF = 2816
CONV_W = 3
EPS = 1e-6

GLA_WIDTH = GLA_HEADS * GLA_DV
FOX_WIDTH = FOX_HEADS * FOX_HD
IN_SPLITS = (GLA_HEADS * GLA_DK, GLA_HEADS * GLA_DK, GLA_WIDTH, GLA_WIDTH, GLA_LOWRANK,
             FOX_WIDTH, FOX_WIDTH, FOX_WIDTH, FOX_HEADS, D_MODEL, D_MODEL)
IN_COLS = sum(IN_SPLITS)
IN_OFFSETS = tuple(int(v) for v in np.cumsum(IN_SPLITS)[:-1])

kernel_name = 'hybrid_gla_fox_convffn_block'


def rmsnorm(x, g):
    xf = x.astype(jnp.float32)
    y = xf * lax.rsqrt(jnp.mean(xf * xf, axis=-1, keepdims=True) + EPS)
    return (y * g.astype(jnp.float32)).astype(x.dtype)


def gla_chunked(q, k, v, log_a):
    B, S, H, dk = q.shape
    dv = v.shape[-1]
    n = S // CHUNK
    def blocks(t):
        return t.astype(jnp.float32).reshape(B, n, CHUNK, H, t.shape[-1]).transpose(0, 3, 1, 2, 4)
    qc, kc, vc, la = blocks(q), blocks(k), blocks(v), blocks(log_a)
    b = jnp.cumsum(la, axis=3)
    q_e = qc * jnp.exp(b)
    k_e = kc * jnp.exp(-b)
    causal = jnp.tril(jnp.ones((CHUNK, CHUNK), dtype=bool))
    scores = jnp.where(causal, jnp.einsum('bhnld,bhnmd->bhnlm', q_e, k_e), 0.0)
    o_intra = jnp.einsum('bhnlm,bhnmv->bhnlv', scores, vc)
    b_last = b[:, :, :, -1:, :]
    chunk_kv = jnp.einsum('bhnld,bhnlv->bhndv', kc * jnp.exp(b_last - b), vc)
    decay = jnp.exp(b_last[:, :, :, 0, :])

    def step(state, inp):
        dec, kv = inp
        return dec[..., None] * state + kv, state

    _, prev = lax.scan(step, jnp.zeros((B, H, dk, dv), jnp.float32),
                       (jnp.moveaxis(decay, 2, 0), jnp.moveaxis(chunk_kv, 2, 0)))
    prev = jnp.moveaxis(prev, 0, 2)
    o = o_intra + jnp.einsum('bhnld,bhndv->bhnlv', q_e, prev)
    return o.transpose(0, 2, 3, 1, 4).reshape(B, S, H, dv).astype(v.dtype)


def fox_attention(q, k, v, log_f):
    B, S, H, d = q.shape
    scale = d ** -0.5
    qh, kh, vh = (t.transpose(0, 2, 1, 3) for t in (q, k, v))
    c = jnp.cumsum(log_f.astype(jnp.float32), axis=1).transpose(0, 2, 1)
    outs = []
    for i in range(S // Q_BLOCK):
        lo, hi = i * Q_BLOCK, (i + 1) * Q_BLOCK
        logits = jnp.einsum('bhqd,bhkd->bhqk', qh[:, :, lo:hi], kh[:, :, :hi]).astype(jnp.float32) * scale
        logits = logits + c[:, :, lo:hi, None] - c[:, :, None, :hi]
        mask = jnp.arange(hi)[None, :] <= jnp.arange(lo, hi)[:, None]
        p = jax.nn.softmax(jnp.where(mask, logits, -jnp.inf), axis=-1)
        outs.append(jnp.einsum('bhqk,bhkd->bhqd', p.astype(v.dtype), vh[:, :, :hi]))
    return jnp.concatenate(outs, axis=2).transpose(0, 2, 1, 3)


def causal_dwconv(u, w, bias):
    S = u.shape[1]
    up = jnp.pad(u, ((0, 0), (CONV_W - 1, 0), (0, 0)))
    out = bias
    for j in range(CONV_W):
        out = out + w[j] * up[:, j:j + S]
    return out


def hybrid_layer(x, norm_mix_g, w_in, gla_alpha_w2, gla_alpha_b, gla_out_norm_g,
                 fox_forget_b, fox_q_norm_g, fox_k_norm_g, gate_b, w_gla_branch,
                 w_fox_branch, w_out, norm_ffn_g, w_up, conv_w, conv_b, w_down):
    B, S, _ = x.shape
    h = rmsnorm(x, norm_mix_g)
    gq, gk, gv, gr, glr, fq, fk, fv, ff, g_gla, g_fox = jnp.split(h @ w_in, IN_OFFSETS, axis=-1)

    log_a = jax.nn.log_sigmoid(glr @ gla_alpha_w2 + gla_alpha_b) / GLA_TAU
    o_gla = gla_chunked(gq.reshape(B, S, GLA_HEADS, GLA_DK) * (GLA_DK ** -0.5),
                        gk.reshape(B, S, GLA_HEADS, GLA_DK),
                        gv.reshape(B, S, GLA_HEADS, GLA_DV),
                        log_a.reshape(B, S, GLA_HEADS, GLA_DK))
    o_gla = rmsnorm(o_gla, gla_out_norm_g.reshape(GLA_HEADS, GLA_DV)).reshape(B, S, GLA_WIDTH) * jax.nn.silu(gr)

    fqh = rmsnorm(fq.reshape(B, S, FOX_HEADS, FOX_HD), fox_q_norm_g)
    fkh = rmsnorm(fk.reshape(B, S, FOX_HEADS, FOX_HD), fox_k_norm_g)
    log_f = jax.nn.log_sigmoid(ff + fox_forget_b)
    o_fox = fox_attention(fqh, fkh, fv.reshape(B, S, FOX_HEADS, FOX_HD), log_f).reshape(B, S, FOX_WIDTH)

    y = jax.nn.sigmoid(g_gla + gate_b[0]) * (o_gla @ w_gla_branch) \
        + jax.nn.sigmoid(g_fox + gate_b[1]) * (o_fox @ w_fox_branch)
    x = x + y @ w_out

    a, val = jnp.split(rmsnorm(x, norm_ffn_g) @ w_up, 2, axis=-1)
    a = causal_dwconv(a, conv_w, conv_b)
    return x + (jax.nn.gelu(a) * val) @ w_down


def setup_inputs(seed: int = 0) -> dict:
    key = jax.random.key(seed)
    ks = jax.random.split(key, 20)
    L = DEPTH
    nrm = jax.random.normal
    return {
        'x': nrm(ks[0], (BATCH, SEQ, D_MODEL), jnp.float32),
        'norm_mix_g': 1.0 + 0.02 * nrm(ks[1], (L, D_MODEL), jnp.float32),
        'w_in': nrm(ks[2], (L, D_MODEL, IN_COLS), jnp.float32) * D_MODEL ** -0.5,
        'gla_alpha_w2': nrm(ks[3], (L, GLA_LOWRANK, GLA_HEADS * GLA_DK), jnp.float32) * GLA_LOWRANK ** -0.5,
        'gla_alpha_b': 0.1 * nrm(ks[4], (L, GLA_HEADS * GLA_DK), jnp.float32),
        'gla_out_norm_g': 1.0 + 0.02 * nrm(ks[5], (L, GLA_WIDTH), jnp.float32),
        'fox_forget_b': 3.0 + nrm(ks[6], (L, FOX_HEADS), jnp.float32),
        'fox_q_norm_g': 1.0 + 0.02 * nrm(ks[7], (L, FOX_HD), jnp.float32),
        'fox_k_norm_g': 1.0 + 0.02 * nrm(ks[8], (L, FOX_HD), jnp.float32),
        'gate_b': 0.1 * nrm(ks[9], (L, 2, D_MODEL), jnp.float32),
        'w_gla_branch': nrm(ks[10], (L, GLA_WIDTH, D_MODEL), jnp.float32) * GLA_WIDTH ** -0.5,
        'w_fox_branch': nrm(ks[11], (L, FOX_WIDTH, D_MODEL), jnp.float32) * FOX_WIDTH ** -0.5,
        'w_out': nrm(ks[12], (L, D_MODEL, D_MODEL), jnp.float32) * D_MODEL ** -0.5,
        'norm_ffn_g': 1.0 + 0.02 * nrm(ks[13], (L, D_MODEL), jnp.float32),
        'w_up': nrm(ks[14], (L, D_MODEL, 2 * D_FF), jnp.float32) * D_MODEL ** -0.5,
        'conv_w': nrm(ks[15], (L, CONV_W, D_FF), jnp.float32) * CONV_W ** -0.5,
        'conv_b': 0.02 * nrm(ks[16], (L, D_FF), jnp.float32),
        'w_down': nrm(ks[17], (L, D_FF, D_MODEL), jnp.float32) * D_FF ** -0.5,
        'norm_final_g': 1.0 + 0.02 * nrm(ks[18], (D_MODEL,), jnp.float32),
    }


def reference(x, norm_mix_g, w_in, gla_alpha_w2, gla_alpha_b, gla_out_norm_g,
              fox_forget_b, fox_q_norm_g, fox_k_norm_g, gate_b, w_gla_branch,
              w_fox_branch, w_out, norm_ffn_g, w_up, conv_w, conv_b, w_down, norm_final_g):
    for l in range(DEPTH):
        x = hybrid_layer(x, norm_mix_g[l], w_in[l], gla_alpha_w2[l], gla_alpha_b[l], gla_out_norm_g[l],
                         fox_forget_b[l], fox_q_norm_g[l], fox_k_norm_g[l], gate_b[l], w_gla_branch[l],
                         w_fox_branch[l], w_out[l], norm_ffn_g[l], w_up[l], conv_w[l], conv_b[l], w_down[l])
    return rmsnorm(x, norm_final_g)
```

```python
from contextlib import ExitStack
import numpy as np
import concourse.bass as bass
import concourse.mybir as mybir
from concourse.bass_utils import run_bass_kernel_spmd

F32 = mybir.dt.float32
BF16 = mybir.dt.bfloat16
AF = mybir.ActivationFunctionType
ALU = mybir.AluOpType

D_MODEL = 1024
BATCH = 16
SEQ = 4096
NCORES = 8
D_FF = 2816
NCC = 22
EPS = 1e-6
NSLAB = 35
SLAB_USED = {0: 192, 7: 2048, 9: 2048, 11: 2048, 13: 2048, 27: 2048, 28: 2048, 31: 3072, 34: 3072}
ENGS = ("pe", "act", "dve", "pool", "sp")


class Op:
    __slots__ = ("eng", "fn", "r", "w", "slot", "deps", "raw", "signal", "sigidx", "target", "waits")

    def __init__(self, eng, fn, r, w, slot):
        self.eng = eng
        self.fn = fn
        self.r = r
        self.w = w
        self.slot = slot
        self.signal = False
        self.sigidx = 0
        self.target = 0
        self.waits = None


class Prog:
    def __init__(self):
        self.ops = []

    def op(self, eng, fn, r=(), w=()):
        self.ops.append(Op(eng, fn, tuple(r), tuple(w), None))

    def dma(self, eng, fn, r=(), w=(), slot=None):
        assert slot is not None
        self.ops.append(Op(eng, fn, tuple(r), tuple(w), slot))

    def analyze(self):
        last_w = {}
        readers = {}
        slot_cnt = {}
        ops = self.ops
        for i, op in enumerate(ops):
            deps = set()
            for k in op.r:
                j = last_w.get(k)
                if j is not None:
                    deps.add(j)
            op.raw = set(deps)
            for k in op.w:
                j = last_w.get(k)
                if j is not None:
                    deps.add(j)
                rd = readers.get(k)
                if rd:
                    deps.update(rd.values())
            deps.discard(i)
            op.deps = deps
            for k in op.w:
                last_w[k] = i
                readers[k] = {}
            pid = ("dma", op.slot) if op.slot is not None else op.eng
            for k in op.r:
                if k in op.w:
                    continue
                readers.setdefault(k, {})[pid] = i
            if op.slot is not None:
                slot_cnt[op.slot] = slot_cnt.get(op.slot, 0) + 1
                op.target = 16 * slot_cnt[op.slot]
        for op in ops:
            for j in op.deps:
                pj = ops[j]
                if pj.slot is None and (pj.eng != op.eng or op.slot is not None or op.eng != "pe"):
                    pj.signal = True
        cnt = {e: 0 for e in ENGS}
        for op in ops:
            if op.slot is None and op.signal:
                cnt[op.eng] += 1
                op.sigidx = cnt[op.eng]
        seen = {e: {} for e in ENGS}
        for op in ops:
            need = {}
            for j in op.deps:
                pj = ops[j]
                if pj.slot is not None:
                    key = ("dma", pj.slot)
                    val = pj.target
                elif pj.eng == op.eng and op.slot is None and op.eng == "pe":
                    continue
                else:
                    key = pj.eng
                    val = pj.sigidx
                if val > need.get(key, 0):
                    need[key] = val
            waits = []
            sd = seen[op.eng]
            for key, val in need.items():
                if sd.get(key, 0) >= val:
                    continue
                sd[key] = val
                waits.append((key, val))
            op.waits = waits
        self.slot_cnt = slot_cnt
        self.sig_cnt = cnt

    def emit(self, nc, final_slots=()):
        self.analyze()
        with ExitStack() as es:
            sems = {}
            for e in ENGS:
                sems[e] = es.enter_context(nc.semaphore("s_" + e))
            for n, s in enumerate(self.slot_cnt):
                sems[("dma", s)] = es.enter_context(nc.semaphore("d%d" % n))
            block = es.enter_context(nc.Block())

            def run(ename):
                def body(e):
                    for op in self.ops:
                        if op.eng != ename:
                            continue
                        for key, val in op.waits:
                            e.wait_ge(sems[key], val)
                        ins = op.fn(e)
                        if op.slot is not None:
                            ins.then_inc(sems[("dma", op.slot)], 16)
                        elif op.signal:
                            ins.then_inc(sems[ename], 1)
                    if ename == "sp":
                        for s in final_slots:
                            e.wait_ge(sems[("dma", s)], 16 * self.slot_cnt[s])
                return body

            block.tensor(run("pe"))
            block.scalar(run("act"))
            block.vector(run("dve"))
            block.gpsimd(run("pool"))
            block.sync(run("sp"))


CW = 1088
PW = 132


class Builder:
    def __init__(self, nseq, S):
        self.nseq = nseq
        self.S = S
        self.NT = S // 128
        self.NST = S // 512
        self.P = Prog()
        self.use_ar = True
        self.debug = False
        self.dbg_done = False
        self.dbg_slots = []
        self.nc = bass.Bass("TRN2", target_bir_lowering=False)

    def _r(self, r, ar=True):
        r = list(r)
        if ar:
            r.append("AR")
        return r

    def mm(self, out, lhsT, rhs, start, stop, r, w):
        self.P.op("pe", lambda e: e.matmul(out, lhsT=lhsT, rhs=rhs, start=start, stop=stop), self._r(r), w)

    def tr(self, out, in_, ident, r, w):
        self.P.op("pe", lambda e: e.transpose(out=out, in_=in_, identity=ident), self._r(r), w)

    def act(self, out, in_, func, r, w, bias=None, scale=None, accum=None):
        kw = {}
        if bias is not None:
            kw["bias"] = bias
        if scale is not None:
            kw["scale"] = scale
        if accum is not None:
            kw["accum_out"] = accum
        self.P.op("act", lambda e: e.activation(out=out, in_=in_, func=func, **kw), self._r(r), w)

    def tt(self, eng, out, in0, in1, op, r, w):
        self.P.op(eng, lambda e: e.tensor_tensor(out=out, in0=in0, in1=in1, op=op), self._r(r), w)

    def ts(self, eng, out, in0, s1, s2, op0, op1, r, w):
        if s2 is None:
            self.P.op(eng, lambda e: e.tensor_scalar(out=out, in0=in0, scalar1=s1, scalar2=None, op0=op0), self._r(r), w)
        else:
            self.P.op(eng, lambda e: e.tensor_scalar(out=out, in0=in0, scalar1=s1, scalar2=s2, op0=op0, op1=op1), self._r(r), w)

    def stt(self, out, in0, scalar, in1, op0, op1, r, w):
        self.P.op("dve", lambda e: e.scalar_tensor_tensor(out=out, in0=in0, scalar=scalar, in1=in1, op0=op0, op1=op1), self._r(r), w)

    def cp(self, eng, out, in_, r, w):
        if eng == "act":
            self.P.op("act", lambda e: e.activation(out=out, in_=in_, func=AF.Copy), self._r(r), w)
        else:
            self.P.op(eng, lambda e: e.tensor_copy(out=out, in_=in_), self._r(r), w)

    def scan(self, out, data0, data1, initial, r, w):
        self.P.op("dve", lambda e: e.tensor_tensor_scan(out=out, data0=data0, data1=data1, initial=initial,
                                                        op0=ALU.mult, op1=ALU.add), self._r(r), w)

    def recip(self, out, in_, r, w):
        self.P.op("dve", lambda e: e.reciprocal(out=out, in_=in_), self._r(r), w)

    def memset(self, eng, ap, val, w, r=()):
        self.P.op(eng, lambda e: e.memset(ap, val), self._r(r), w)

    def dma(self, eng, out, in_, r, w, slot, ar=True):
        self.P.dma(eng, lambda e: e.dma_start(out=out, in_=in_), self._r(r, ar), w, slot)

    def dbg(self, name, ap, rkeys):
        if not self.debug or self.dbg_done:
            return
        t = self.nc.dram_tensor("dbg_" + name, list(ap.shape), ap.dtype, kind="ExternalOutput").ap()
        self.dma("sp", t, ap, r=rkeys, w=[], slot=("dbg", name), ar=False)
        self.dbg_slots.append(("dbg", name))

    def barrier(self):
        self.P.op("dve", lambda e: e.memset(self.DUMMY[:, 0:1], 0.0), (), ("AR",))

    def psum(self):
        if self.ps_wide:
            i = self.ps_rot6
            self.ps_rot6 = (self.ps_rot6 + 1) % 6
            return self.PSALL[i]
        i = self.ps_rot
        self.ps_rot = (self.ps_rot + 1) % 4
        return self.PS[i], ("PS", i)

    def wnext(self):
        n = self.w_used
        total = self.w_total
        while self.w_issued < min(n + 2, total):
            m = self.w_issued
            slab = m % NSLAB
            slot = m % 3
            wd = SLAB_USED.get(slab, 4096)
            self.dma("sp", self.WS[slot][:, 0:wd], self.wbf[slab][:, 0:wd], r=[("WBF", slab)], w=[("WS", slot)],
                     slot=("w", slot), ar=False)
            self.w_issued += 1
        self.w_used += 1
        slot = n % 3
        return self.WS[slot], ("WS", slot)

    def build(self):
        nc = self.nc
        nseq, S, NT, NST = self.nseq, self.S, self.NT, self.NST
        ntok = nseq * S
        dt = nc.dram_tensor
        self.x = dt("x", [ntok, D_MODEL], F32, kind="ExternalInput").ap()
        self.wsl = dt("wsl", [NSLAB, 128, 4096], F32, kind="ExternalInput").ap()
        self.cst = dt("cst", [128, CW], F32, kind="ExternalInput").ap()
        self.par = dt("par", [128, PW], F32, kind="ExternalInput").ap()
        self.w2 = dt("w2", [16, 256], F32, kind="ExternalInput").ap()
        self.gfin = dt("gfin", [D_MODEL], F32, kind="ExternalInput").ap()
        self.y = dt("y", [ntok, D_MODEL], F32, kind="ExternalOutput").ap()
        self.wbf = dt("wbf", [NSLAB, 128, 4096], BF16, kind="Internal").ap()
        self.kcache = dt("kcache", [8, 65, S], BF16, kind="Internal").ap()
        self.vcache = dt("vcache", [8, 128, NT, 65], BF16, kind="Internal").ap()

        es = ExitStack()
        sb = lambda n, s, d: es.enter_context(nc.sbuf_tensor(n, s, d))
        ps = lambda n, s, d: es.enter_context(nc.psum_tensor(n, s, d))
        self.CST = sb("CST", [128, CW], F32)
        self.CB = sb("CB", [128, 384], BF16)
        self.PAR = sb("PAR", [128, PW], F32)
        self.DER = sb("DER", [128, 8], F32)
        self.GFIN = sb("GFIN", [128, D_MODEL], F32)
        self.W2f = sb("W2f", [16, 256], F32)
        self.W2b = sb("W2b", [16, 256], BF16)
        self.NEGC = sb("NEGC", [128, NT, 8], F32)
        self.X = sb("X", [128, 4, D_MODEL], F32)
        self.XS = sb("XS", [128, 4, D_MODEL], BF16)
        self.XT = sb("XT", [128, 8, 512], BF16)
        self.JUNK = sb("JUNK", [128, 128], BF16)
        self.WS = [sb("WS%d" % i, [128, 4096], BF16) for i in range(3)]
        self.KTN = sb("KTN", [128, 8, 512], BF16)
        self.VTN = sb("VTN", [128, 8, 4, 65], BF16)
        self.KH = [sb("KH%d" % i, [128, S], BF16) for i in range(2)]
        self.VH = [sb("VH%d" % i, [128, NT * 65 + 64], BF16) for i in range(2)]
        self.YT = sb("YT", [128, 8, 512], BF16)
        self.OUTB = [sb("OUTB%d" % i, [128, D_MODEL], F32) for i in range(2)]
        self.ACC = sb("ACC", [128, 8], F32)
        self.ACC2 = sb("ACC2", [128, 8], F32)
        self.RINV = sb("RINV", [128, 8], F32)
        self.SST = sb("SST", [128, 2, 128], F32)
        self.CARRY = sb("CARRY", [128, NCC, 2], F32)
        self.CPC = sb("CPC", [128, 1], F32)
        self.NBL = sb("NBL", [128, 8], F32)
        self.DUMMY = sb("DUMMY", [128, 2], F32)
        self.EPSC = sb("EPSC", [128, 1], F32)
        self.bd_ones = sb("BDONES", [128, 128], BF16)
        NAR = 21696
        self.AR = sb("ARENA", [128, NAR], F32)
        self.PS = [ps("PS%d" % i, [128, 512], F32) for i in range(5)]
        self.PACC = ps("PACC", [128, 512], F32)
        self.PB = [ps("PB%d" % i, [128, 512], BF16) for i in range(2)]
        self.ps_rot = 0
        self.ACCS = [(self.PACC, "PACC"), (self.PS[4], ("PS", 4))]
        self.acc_rot = 0
        self.ps_wide = True
        self.ps_rot6 = 0
        self.PSALL = [(self.PS[i], ("PS", i)) for i in range(5)] + [(self.PACC, "PACC")]

        off = [0]

        def a_f32(n):
            o = off[0]
            off[0] += n
            assert off[0] <= NAR, off[0]
            return self.AR[:, o:o + n]

        def a_bf16(n):
            w = (n + 1) // 2
            o = off[0]
            off[0] += w
            assert off[0] <= NAR, off[0]
            return self.AR[:, o:o + w].bitcast(BF16)

        self.GLRT = a_bf16(512)
        self.LFA = a_f32(512)
        self.LFB = a_f32(512)
        self.CP = a_f32(512)
        self.CNb = a_bf16(512)
        self.SP = a_f32(512)
        self.Bc = a_f32(1024).rearrange("p (c t) -> p c t", c=2)
        self.EQ = a_f32(1024).rearrange("p (c t) -> p c t", c=2)
        self.EK = a_bf16(1024).rearrange("p (c t) -> p c t", c=2)
        self.EKD = a_bf16(1024).rearrange("p (c t) -> p c t", c=2)
        self.QE = a_bf16(1024).rearrange("p (c t) -> p c t", c=2)
        self.KE = a_bf16(1024).rearrange("p (c t) -> p c t", c=2)
        self.KD = a_bf16(1024).rearrange("p (c t) -> p c t", c=2)
        self.VG = a_bf16(2048).rearrange("p (a t) -> p a t", a=4)
        self.SG = a_bf16(2048).rearrange("p (a t) -> p a t", a=4)
        self.SC = [a_bf16(128) for _ in range(2)]
        self.KDT8 = [a_bf16(128) for _ in range(8)]
        self.SSB4 = [a_bf16(256).rearrange("p (c v) -> p c v", c=2) for _ in range(4)]
        self.OGN = [a_bf16(512) for _ in range(2)]
        self.OGT = a_bf16(2048).rearrange("p (a t) -> p a t", a=4)
        self.SQ = [a_bf16(512) for _ in range(3)]
        self.RB = [a_f32(512) for _ in range(2)]
        self.QT = a_bf16(4096).rearrange("p (h t) -> p h t", h=8)
        self.PT = [a_bf16(512) for _ in range(3)]
        self.OS = a_f32(512)
        self.LI = a_f32(512)
        self.OFT = a_bf16(2048).rearrange("p (c t) -> p c t", c=4)
        self.OSH = [a_bf16(512) for _ in range(2)]
        self.SGG = [a_f32(512) for _ in range(2)]
        self.MT = [a_f32(512) for _ in range(2)]
        a_end = off[0]
        off[0] = 0
        self.GT = a_bf16(NCC * 512).rearrange("p (c t) -> p c t", c=NCC)
        self.ACV = [a_f32(516) for _ in range(2)]
        self.U = [a_f32(512) for _ in range(2)]
        off[0] = 0
        self.STG = [a_f32(4096) for _ in range(2)]

        self.ident_f = self.CST[:, 0:128]
        self.trimask = self.CST[:, 128:256]
        self.ones_f = self.CST[:, 384:512]
        self.sel65 = self.CST[:, 512:576]
        self.ones512 = self.CST[:, 576:1088]
        self.kv_cnt = 0
        self.x_preloaded = False
        self.fox_pending = None
        self.ident_b = self.CB[:, 0:128]
        self.causal_b = self.CB[:, 128:256]
        self.ones_b = self.CB[:, 256:384]

        self.prologue()
        self.w_used = 0
        self.w_issued = 0
        self.w_total = NSLAB * nseq * NST
        self.out_cnt = 0
        for q in range(nseq):
            self.seq_init()
            for st in range(NST):
                self.supertile(q, st)
        self.P.emit(nc, final_slots=[("out", 0), ("out", 1)] + self.dbg_slots)
        es.close()
        return nc

    def prologue(self):
        self.dma("pool", self.CST[:, :], self.cst[:, :], r=[], w=["CST"], slot="lcst")
        self.dma("pool", self.PAR[:, :], self.par[:, :], r=[], w=["PAR"], slot="lpar")
        self.dma("pool", self.W2f[0:16, :], self.w2[:, :], r=[], w=["W2f"], slot="lw2")
        self.dma("pool", self.GFIN[:, :], self.gfin.partition_broadcast(128), r=[], w=["GFIN"], slot="lgf")
        self.cp("dve", self.CB[:, 0:128], self.CST[:, 0:128], r=["CST"], w=["CB"])
        self.cp("dve", self.CB[:, 128:256], self.CST[:, 256:384], r=["CST"], w=["CB"])
        self.cp("dve", self.CB[:, 256:384], self.CST[:, 384:512], r=["CST"], w=["CB"])
        self.cp("dve", self.W2b[0:16, :], self.W2f[0:16, :], r=["W2f"], w=["W2b"])
        self.ts("dve", self.DER[:, 0:2], self.PAR[:, 124:126], -1.0, None, ALU.mult, None, r=["PAR"], w=["DER"])
        self.ts("dve", self.DER[0:8, 2:3], self.PAR[0:8, 126:127], -1.0, None, ALU.mult, None, r=["PAR"], w=["DER"])
        self.ts("dve", self.DER[:, 3:4], self.PAR[:, 127:128], 0.125, None, ALU.mult, None, r=["PAR"], w=["DER"])
        self.memset("dve", self.EPSC[:, :], EPS, w=["EPSC"])
        self.memset("dve", self.bd_ones[:, :], 0.0, w=["BD"])
        self.memset("dve", self.bd_ones[0:64, 0:64], 1.0, w=["BD"])
        self.memset("dve", self.bd_ones[64:128, 64:128], 1.0, w=["BD"])
        for n in range(NSLAB):
            s2 = n % 2
            s3 = n % 3
            wd = SLAB_USED.get(n, 4096)
            self.dma("sp", self.STG[s2][:, 0:wd], self.wsl[n][:, 0:wd], r=[], w=[("STG", s2)], slot=("pin", s2))
            eng = "act" if n % 2 == 0 else "dve"
            self.cp(eng, self.WS[s3][:, 0:wd], self.STG[s2][:, 0:wd], r=[("STG", s2)], w=[("WS", s3)])
            self.dma("pool", self.wbf[n][:, 0:wd], self.WS[s3][:, 0:wd], r=[("WS", s3)], w=[("WBF", n)], slot=("pout", s3))
        self.barrier()
        self.memset("dve", self.KTN[64:65, :, :], 1.0, w=["KTN1"])
        self.memset("dve", self.VTN[:, :, :, 64:65], 1.0, w=["VTN1"])
        for b_ in range(2):
            self.memset("pool", self.VH[b_][:, :], 0.0, w=[("VH", b_)])

    def seq_init(self):
        self.memset("dve", self.SST[:, :, :], 0.0, w=["SST"])
        self.memset("dve", self.CARRY[:, :, :], 0.0, w=["CARRY"])
        self.memset("dve", self.CPC[:, :], 0.0, w=["CPC"])

    def rinv4(self, n_inv, ncols):
        k4 = lambda n: [n] + [(n, t) for t in range(4)]
        self.act(self.ACC2[:, 0:ncols], self.ACC[:, 0:ncols], AF.Ln, r=k4("ACC") + ["EPSC"], w=k4("ACC2"), bias=self.EPSC[:, 0:1], scale=n_inv)
        self.act(self.RINV[:, 0:ncols], self.ACC2[:, 0:ncols], AF.Exp, r=k4("ACC2"), w=k4("RINV"), scale=-0.5)

    def norm_pre(self, tt):
        self.act(self.XS[:, tt, :], self.X[:, tt, :], AF.Square, r=[("X", tt)], w=[("XS", tt), ("ACC", tt)],
                 accum=self.ACC[:, tt:tt + 1])
        self.act(self.ACC2[:, tt:tt + 1], self.ACC[:, tt:tt + 1], AF.Ln, r=[("ACC", tt), "EPSC"], w=[("ACC2", tt)],
                 bias=self.EPSC[:, 0:1], scale=1.0 / D_MODEL)
        self.act(self.RINV[:, tt:tt + 1], self.ACC2[:, tt:tt + 1], AF.Exp, r=[("ACC2", tt)], w=[("RINV", tt)], scale=-0.5)
        self.ts("dve", self.XS[:, tt, :], self.X[:, tt, :], self.RINV[:, tt:tt + 1], None, ALU.mult, None,
                r=[("X", tt), ("RINV", tt)], w=[("XS", tt)])

    def norm_tr(self, tt, gcol):
        for g4 in range(2):
            pb = self.PB[g4]
            pk = ("PB", g4)
            for k in range(4):
                kc = g4 * 4 + k
                self.tr(pb[:, k * 128:(k + 1) * 128], self.XS[:, tt, kc * 128:(kc + 1) * 128], self.ident_b,
                        r=[("XS", tt), "CB"], w=[pk])
            gain = self.PAR[:, gcol + g4 * 4:gcol + g4 * 4 + 4].unsqueeze(2).to_broadcast([128, 4, 128])
            self.tt("dve", self.XT[:, g4 * 4:(g4 + 1) * 4, tt * 128:(tt + 1) * 128],
                    pb[:, :].rearrange("p (a t) -> p a t", a=4), gain, ALU.mult,
                    r=[pk, "PAR"], w=[("XT", kc_) for kc_ in range(g4 * 4, g4 * 4 + 4)])

    def norm_tile(self, tt, gcol):
        self.norm_pre(tt)
        self.norm_tr(tt, gcol)

    def fm_proj(self, out, okey, slab8, skey, col0, M):
        xr = [("XT", kc) for kc in range(8)]
        for kc in range(8):
            self.mm(out, slab8[:, kc, col0:col0 + M], self.XT[:, kc, :], kc == 0, kc == 7,
                    r=[skey, ("XT", kc)], w=[okey])

    def tm_proj(self, out, okey, slab8, skey, col0, N, tt):
        for kc in range(8):
            self.mm(out, self.XT[:, kc, tt * 128:(tt + 1) * 128], slab8[:, kc, col0:col0 + N], kc == 0, kc == 7,
                    r=[skey, ("XT", kc)], w=[okey])

    @staticmethod
    def v8(slab, ncol=512):
        return slab[:, 0:8 * ncol].rearrange("p (k c) -> p k c", k=8)

    def xload(self, row0, tt):
        self.dma("pool", self.X[:, tt, :], self.x[row0 + tt * 128:row0 + (tt + 1) * 128, :],
                 r=[], w=[("X", tt)], slot=("lx", tt), ar=False)

    def supertile(self, q, st):
        S, NT = self.S, self.NT
        row0 = q * S + st * 512
        if not self.x_preloaded:
            for tt in range(4):
                self.xload(row0, tt)
        self.x_preloaded = False
        for tt in range(4):
            self.norm_tile(tt, 0)
        self.dbg("XT", self.XT[:, :, :], [("XT", k) for k in range(8)])

        slab, sk = self.wnext()
        s8 = self.v8(slab, 24)
        p, pk = self.psum()
        self.fm_proj(p[0:16, :], pk, s8, sk, 0, 16)
        self.cp("act", self.GLRT[0:16, :], p[0:16, :], r=[pk], w=["GLRT"])
        p, pk = self.psum()
        self.fm_proj(p[0:8, :], pk, s8, sk, 16, 8)
        self.act(self.LFA[0:8, :], p[0:8, :], AF.Exp, r=[pk, "DER"], w=["LFA"], bias=self.DER[0:8, 2:3], scale=-1.0)
        self.act(self.LFB[0:8, :], self.LFA[0:8, :], AF.Ln, r=["LFA"], w=["LFB"], bias=1.0)
        self.scan(self.CP[0:8, :], self.ones512[0:8, :], self.LFB[0:8, :], self.CPC[0:8, 0:1],
                  r=["LFB", "CPC", "CST"], w=["CP"])
        self.cp("dve", self.CPC[0:8, 0:1], self.CP[0:8, 511:512], r=["CP"], w=["CPC"])
        self.ts("dve", self.CNb[0:8, :], self.CP[0:8, :], -1.0, None, ALU.mult, None, r=["CP"], w=["CNb"])
        for h in range(8):
            self.dma("pool", self.QT[64:65, h, :], self.CNb[h:h + 1, :], r=["CNb"], w=[("QTc", h)], slot=("crow", h))
        for tt in range(4):
            p, pk = self.psum()
            self.tr(p[:, 0:8], self.CP[0:8, tt * 128:(tt + 1) * 128], self.ident_f[0:8, 0:8], r=["CP", "CST"], w=[pk])
            self.cp("dve", self.NEGC[:, st * 4 + tt, :], p[:, 0:8], r=[pk], w=[("NEGC", st * 4 + tt)])

        slab, sk = self.wnext()
        s8 = self.v8(slab)
        for c in range(2):
            p, pk = self.psum()
            self.mm(p[:, :], self.W2b[0:16, c * 128:(c + 1) * 128], self.GLRT[0:16, :], True, True,
                    r=["W2b", "GLRT"], w=[pk])
            self.act(self.SP[:, :], p[:, :], AF.Exp, r=[pk, "DER"], w=["SP"], bias=self.DER[:, c:c + 1], scale=-1.0)
            self.act(self.SP[:, :], self.SP[:, :], AF.Ln, r=["SP"], w=["SP"], bias=1.0)
            for tt in range(4):
                self.scan(self.Bc[:, c, tt * 128:(tt + 1) * 128], self.ones_f[:, :], self.SP[:, tt * 128:(tt + 1) * 128],
                          0.0, r=["SP", "CST"], w=[("Bc", c)])
            self.act(self.EQ[:, c, :], self.Bc[:, c, :], AF.Exp, r=[("Bc", c)], w=[("EQ", c)], scale=-1.0 / 16)
            self.act(self.EK[:, c, :], self.Bc[:, c, :], AF.Exp, r=[("Bc", c)], w=[("EK", c)], scale=1.0 / 16)
            self.ts("dve", self.NBL[:, c * 4:(c + 1) * 4], self.Bc[:, c, 127::128], -1.0 / 16, None, ALU.mult, None,
                    r=[("Bc", c)], w=["NBL"])
            for tt in range(4):
                self.act(self.EKD[:, c, tt * 128:(tt + 1) * 128], self.Bc[:, c, tt * 128:(tt + 1) * 128], AF.Exp,
                         r=[("Bc", c), "NBL"], w=[("EKD", c)], bias=self.NBL[:, c * 4 + tt:c * 4 + tt + 1], scale=1.0 / 16)
            p, pk = self.psum()
            self.fm_proj(p[:, :], pk, s8, sk, c * 128, 128)
            self.stt(self.QE[:, c, :], p[:, :], 0.125, self.EQ[:, c, :], ALU.mult, ALU.mult, r=[pk, ("EQ", c)], w=[("QE", c)])
            p, pk = self.psum()
            self.fm_proj(p[:, :], pk, s8, sk, 256 + c * 128, 128)
            self.tt("dve", self.KE[:, c, :], p[:, :], self.EK[:, c, :], ALU.mult, r=[pk, ("EK", c)], w=[("KE", c)])
            self.tt("dve", self.KD[:, c, :], p[:, :], self.EKD[:, c, :], ALU.mult, r=[pk, ("EKD", c)], w=[("KD", c)])
        slab, sk = self.wnext()
        s8 = self.v8(slab)
        for tt in range(4):
            p, pk = self.psum()
            self.tm_proj(p[:, :], pk, s8, sk, 0, 512, tt)
            self.cp("act", self.VG[:, tt, :], p[:, :], r=[pk], w=[("VG", tt)])
        self.gla_prep()
        slab, sk = self.wnext()
        s8 = self.v8(slab)
        for tt in range(4):
            p, pk = self.psum()
            self.tm_proj(p[:, :], pk, s8, sk, 0, 512, tt)
            self.act(self.SG[:, tt, :], p[:, :], AF.Silu, r=[pk], w=[("SG", tt)])

        self.dbg("QE", self.QE[:, :, :], [("QE", 0), ("QE", 1)])
        self.dbg("KE", self.KE[:, :, :], [("KE", 0), ("KE", 1)])
        self.dbg("KD", self.KD[:, :, :], [("KD", 0), ("KD", 1)])
        self.dbg("VG", self.VG[:, :, :], [("VG", t) for t in range(4)])
        self.dbg("SG", self.SG[:, :, :], [("SG", t) for t in range(4)])
        self.dbg("CP", self.CP[0:8, :], ["CP"])

        self.ps_wide = True
        for which in range(2):
            slab, sk = self.wnext()
            s8 = self.v8(slab)
            gvec = self.DER[:, 3:4] if which == 0 else self.PAR[:, 128:129]
            gkey = "DER" if which == 0 else "PAR"

            def dst(h):
                if which == 0:
                    return self.QT[0:64, h, :], ("QT", h)
                return self.KTN[0:64, h, :], ("KTN", h)

            def stage_a(pr):
                p, pk = self.psum()
                self.fm_proj(p[:, :], pk, s8, sk, pr * 128, 128)
                self.act(self.SQ[pr % 3][:, :], p[:, :], AF.Square, r=[pk], w=[("SQ", pr % 3)])
                self.ts("dve", self.PT[pr % 3][:, :], p[:, :], gvec, None, ALU.mult, None,
                        r=[pk, gkey, ("SQ", pr % 3)], w=[("PT", pr % 3)])

            def stage_b(pr):
                i = pr % 2
                p2, pk2 = self.psum()
                self.mm(p2[:, :], self.bd_ones[:, :], self.SQ[pr % 3][:, :], True, True, r=["BD", ("SQ", pr % 3)], w=[pk2])
                self.act(self.RB[i][:, :], p2[:, :], AF.Ln, r=[pk2, "EPSC"], w=[("RB", i)], bias=self.EPSC[:, 0:1], scale=1.0 / 64)
                self.act(self.RB[i][:, :], self.RB[i][:, :], AF.Exp, r=[("RB", i)], w=[("RB", i)], scale=-0.5)
                d0, dk0 = dst(2 * pr)
                d1, dk1 = dst(2 * pr + 1)
                j3 = pr % 3
                self.tt("dve", d0, self.PT[j3][0:64, :], self.RB[i][0:64, :], ALU.mult, r=[("PT", j3), ("RB", i)], w=[dk0])
                self.tt("dve", self.PT[j3][64:128, :], self.PT[j3][64:128, :], self.RB[i][64:128, :], ALU.mult,
                        r=[("PT", j3), ("RB", i)], w=[("PT", j3)])
                self.dma("pool", d1, self.PT[j3][64:128, :], r=[("PT", j3)], w=[dk1], slot=("hshift", which * 4 + pr))

            stage_a(0)
            stage_a(1)
            for pr in range(4):
                if pr + 2 < 4:
                    stage_a(pr + 2)
                stage_b(pr)
        self.ps_wide = True
        self.dma("pool", self.kcache[:, :, st * 512:(st + 1) * 512].rearrange("h d s -> d h s"), self.KTN[0:65, :, :],
                 r=[("KTN", h) for h in range(8)] + ["KTN1"], w=[("KDR", st)], slot="kst")
        slab, sk = self.wnext()
        s8 = self.v8(slab)
        for tt in range(4):
            p, pk = self.psum()
            self.tm_proj(p[:, :], pk, s8, sk, 0, 512, tt)
            self.cp("act", self.VTN[:, :, tt, 0:64], p[:, :].rearrange("p (h d) -> p h d", h=8), r=[pk], w=[("VTN", tt)])
        self.dma("pool", self.vcache[:, :, st * 4:(st + 1) * 4, :].rearrange("h p j d -> p h (j d)"),
                 self.VTN[:, :, :, :].rearrange("p h j d -> p h (j d)"),
                 r=[("VTN", t) for t in range(4)] + ["VTN1"], w=[("VDR", st)], slot="vst")

        self.dbg("QT", self.QT[0:65, :, :], [("QT", h) for h in range(8)] + [("QTc", h) for h in range(8)])
        self.dbg("KTN", self.KTN[0:65, :, :], [("KTN", h) for h in range(8)] + ["KTN1"])
        self.dbg("VTN", self.VTN[:, :, :, :], [("VTN", t) for t in range(4)] + ["VTN1"])
        self.dbg("NEGC", self.NEGC[:, 0:4, :], [("NEGC", t) for t in range(4)])
        self.fox_first = self.kv_load(st, 0)
        self.gla_tiles()
        self.dbg("OGT", self.OGT[:, :, :], ["OGT"])
        self.fox(st)
        self.dbg("OFT", self.OFT[:, :, :], [("OFT", h) for h in range(8)])

        self.merge()

        self.dbg("X1", self.X[:, :, :], [("X", t) for t in range(4)])
        self.barrier()
        self.ffn()
        self.dbg("GT", self.GT[:, :, :], [("GT", c) for c in range(NCC)])
        self.dbg("X2", self.X[:, :, :], [("X", t) for t in range(4)])
        nrow = row0 + 512 if row0 + 512 < self.nseq * self.S else None
        self.final_norm(row0, nrow)
        self.barrier()
        self.dbg_done = True

    def kv_load(self, st, h):
        b = self.kv_cnt % 2
        self.kv_cnt += 1
        nk = (st + 1) * 512
        nb = (st + 1) * 4
        deps_k = [("KDR", s) for s in range(st + 1)]
        deps_v = [("VDR", s) for s in range(st + 1)]
        self.dma("sp", self.KH[b][0:65, 0:nk], self.kcache[h, :, 0:nk], r=deps_k, w=[("KH", b)], slot=("kh", b), ar=False)
        self.dma("sp", self.VH[b][:, 0:nb * 65], self.vcache[h, :, 0:nb, :].rearrange("p j d -> p (j d)"), r=deps_v, w=[("VH", b)], slot=("vh", b), ar=False)
        return b

    def gla_prep(self):
        for tt in range(4):
            ts_ = slice(tt * 128, (tt + 1) * 128)
            for c in range(2):
                idx = tt * 2 + c
                pb = self.PB[idx % 2]
                pbk = ("PB", idx % 2)
                self.tr(pb[:, 0:128], self.KD[:, c, ts_], self.ident_b, r=[("KD", c), "CB"], w=[pbk])
                self.cp("act" if idx % 2 == 0 else "dve", self.KDT8[idx][:, :], pb[:, 0:128], r=[pbk], w=[("KDT", idx)])
        for tt in range(4):
            for c in range(2):
                idx = tt * 2 + c
                self.mm(self.PS[tt][:, c * 256:(c + 1) * 256], self.KDT8[idx][:, :], self.VG[:, tt, c * 256:(c + 1) * 256],
                        True, True, r=[("KDT", idx), ("VG", tt)], w=[("PS", tt)])
        for tt in range(4):
            self.cp("dve", self.SSB4[tt][:, :, :], self.SST[:, :, :], r=["SST"], w=[("SSB", tt)])
            col = tt * 128 + 127
            for c in range(2):
                for hp in range(2):
                    r0 = hp * 64
                    self.stt(self.SST[r0:r0 + 64, c, :], self.SST[r0:r0 + 64, c, :], self.EQ[r0:r0 + 64, c, col:col + 1],
                             self.PS[tt][r0:r0 + 64, c * 256 + hp * 128:c * 256 + (hp + 1) * 128], ALU.mult, ALU.add,
                             r=["SST", ("EQ", c), ("PS", tt)], w=["SST"])
    def gla_tiles(self):
        self.ps_wide = False
        def tile_mm(tt):
            ts_ = slice(tt * 128, (tt + 1) * 128)
            acc, acck = self.ACCS[self.acc_rot % 2]
            self.acc_rot += 1

            def sc(h):
                c = h // 2
                b0 = (h % 2) * 64
                p, pk = self.psum()
                self.mm(p[:, 0:128], self.KE[b0:b0 + 64, c, ts_], self.QE[b0:b0 + 64, c, ts_], True, True,
                        r=[("KE", c), ("QE", c)], w=[pk])
                self.tt("dve", self.SC[h % 2][:, :], p[:, 0:128], self.trimask, ALU.mult, r=[pk, "CST"], w=[("SC", h % 2)])

            def oo(h):
                c = h // 2
                b0 = (h % 2) * 64
                hs = slice(h * 128, (h + 1) * 128)
                self.mm(acc[:, hs], self.SC[h % 2][:, :], self.VG[:, tt, hs], True, False,
                        r=[("SC", h % 2), ("VG", tt)], w=[acck])
                self.mm(acc[:, hs], self.QE[b0:b0 + 64, c, ts_], self.SSB4[tt][b0:b0 + 64, c, :], False, True,
                        r=[("QE", c), ("SSB", tt)], w=[acck])

            sc(0)
            for h in range(4):
                if h + 1 < 4:
                    sc(h + 1)
                oo(h)
            return acc, acck

        def tile_post(tt, acc, acck):
            ts_ = slice(tt * 128, (tt + 1) * 128)
            for h in range(4):
                self.act(self.OGN[tt % 2][:, h * 128:(h + 1) * 128], acc[:, h * 128:(h + 1) * 128], AF.Square, r=[acck],
                         w=[("OGN", tt % 2), "ACC"] + [("ACC", t) for t in range(4)], accum=self.ACC[:, h:h + 1])
            self.rinv4(1.0 / 128, 4)
            for h in range(4):
                hs = slice(h * 128, (h + 1) * 128)
                self.stt(self.OGN[tt % 2][:, hs], acc[:, hs], self.RINV[:, h:h + 1], self.SG[:, tt, hs], ALU.mult, ALU.mult,
                         r=[acck, "RINV", ("SG", tt)], w=[("OGN", tt % 2)])

        def tile_tr(tt):
            ts_ = slice(tt * 128, (tt + 1) * 128)
            pb = self.PB[tt % 2]
            pbk = ("PB", tt % 2)
            for c4 in range(4):
                self.tr(pb[:, c4 * 128:(c4 + 1) * 128], self.OGN[tt % 2][:, c4 * 128:(c4 + 1) * 128], self.ident_b,
                        r=[("OGN", tt % 2), "CB"], w=[pbk])
            self.tt("dve", self.OGT[:, :, ts_], pb[:, :].rearrange("p (a t) -> p a t", a=4),
                    self.PAR[:, 16:20].unsqueeze(2).to_broadcast([128, 4, 128]), ALU.mult,
                    r=[pbk, "PAR"], w=["OGT"])

        cur = tile_mm(0)
        for tt in range(4):
            nxt = tile_mm(tt + 1) if tt + 1 < 4 else None
            tile_post(tt, *cur)
            tile_tr(tt)
            cur = nxt
        self.ps_wide = True

    def fox(self, st):
        nblk = (st + 1) * 4
        self.ps_wide = False
        nxt = self.fox_first
        for h in range(8):
            b = nxt
            if h + 1 < 8:
                nxt = self.kv_load(st, h + 1)
            KH, VH = self.KH[b], self.VH[b]
            qr = [("QT", h), ("QTc", h)]
            acc, acck = self.ACCS[self.acc_rot % 2]
            self.acc_rot += 1

            def qk(j):
                diag = j >= 4 * st
                q0 = (j - 4 * st) * 128 if diag else 0
                p, pk = self.psum()
                self.mm(p[:, q0:512], KH[0:65, j * 128:(j + 1) * 128], self.QT[0:65, h, q0:512], True, not diag,
                        r=[("KH", b)] + qr, w=[pk])
                if diag:
                    self.mm(p[:, q0:q0 + 128], self.ident_b, self.causal_b, False, True, r=["CB"], w=[pk])
                return p, pk, q0

            def pv(j, p, pk, q0):
                i = j % 3
                self.act(self.PT[i][:, q0:512], p[:, q0:512], AF.Exp, r=[pk, ("NEGC", j)], w=[("PT", i)],
                         bias=self.NEGC[:, j, h:h + 1])
                self.mm(acc[:, q0:512], VH[:, j * 65:j * 65 + 128], self.PT[i][:, q0:512], j == 0, j == nblk - 1,
                        r=[("VH", b), ("PT", i)], w=[acck])

            def normalize(h_, acc_, acck_):
                self.cp("act", self.OS[0:65, :], acc_[0:65, :], r=[acck_], w=["OS"])
                p_, pk_ = self.psum()
                self.mm(p_[0:64, :], self.sel65[0:65, 0:64], self.OS[0:65, :], True, True, r=["CST", "OS"], w=[pk_])
                self.recip(self.LI[0:64, :], p_[0:64, :], r=[pk_], w=["LI"])
                if h_ % 2 == 0:
                    self.tt("pool", self.OFT[0:64, h_ // 2, :], self.OS[0:64, :], self.LI[0:64, :], ALU.mult,
                            r=["OS", "LI"], w=[("OFT", h_)])
                else:
                    k_ = (h_ // 2) % 2
                    self.tt("pool", self.OSH[k_][0:64, :], self.OS[0:64, :], self.LI[0:64, :], ALU.mult,
                            r=["OS", "LI"], w=[("OSH", k_)])
                    self.dma("pool", self.OFT[64:128, h_ // 2, :], self.OSH[k_][0:64, :], r=[("OSH", k_)], w=[("OFT", h_)],
                             slot=("oshift", h_ // 2))

            DEPTH = 2
            pend = [qk(j) for j in range(min(DEPTH, nblk))]
            if self.fox_pending is not None:
                normalize(*self.fox_pending)
            for j in range(nblk):
                if j + DEPTH < nblk:
                    pend.append(qk(j + DEPTH))
                pv(j, *pend.pop(0))
            self.fox_pending = (h, acc, acck)
        normalize(*self.fox_pending)
        self.fox_pending = None
        self.ps_wide = True

    def merge(self):
        for which in range(2):
            for grp in range(2):
                wslab, wk = self.wnext()
                gslab, gsk = self.wnext()
                gs8 = self.v8(gslab)
                W = wslab[:, 0:2048].rearrange("p (k c) -> p k c", k=4)
                pys = {}

                def py_mm(f):
                    py, pyk = self.psum()
                    if which == 0:
                        for c4 in range(4):
                            self.mm(py[:, :], W[:, c4, f * 128:(f + 1) * 128], self.OGT[:, c4, :], c4 == 0, c4 == 3,
                                    r=[wk, "OGT"], w=[pyk])
                    else:
                        for c4 in range(4):
                            self.mm(py[:, :], W[:, c4, f * 128:(f + 1) * 128], self.OFT[:, c4, :], c4 == 0, c4 == 3,
                                    r=[wk, ("OFT", 2 * c4), ("OFT", 2 * c4 + 1)], w=[pyk])
                    pys[f] = (py, pyk)

                def gate(f):
                    fc = grp * 4 + f
                    i = fc % 2
                    py, pyk = pys.pop(f)
                    pg, pgk = self.psum()
                    self.fm_proj(pg[:, :], pgk, gs8, gsk, f * 128, 128)
                    bcol = (20 if which == 0 else 28) + fc
                    self.act(self.SGG[i][:, :], pg[:, :], AF.Sigmoid, r=[pgk, "PAR"], w=[("SGG", i)], bias=self.PAR[:, bcol:bcol + 1])
                    if which == 0:
                        self.tt("dve", self.YT[:, fc, :], py[:, :], self.SGG[i][:, :], ALU.mult, r=[pyk, ("SGG", i)], w=[("YT", fc)])
                    else:
                        self.tt("dve", self.MT[i][:, :], py[:, :], self.SGG[i][:, :], ALU.mult, r=[pyk, ("SGG", i)], w=[("MT", i)])
                        self.tt("pool", self.YT[:, fc, :], self.YT[:, fc, :], self.MT[i][:, :], ALU.add,
                                r=[("YT", fc), ("MT", i)], w=[("YT", fc)])

                py_mm(0)
                py_mm(1)
                gate(0)
                gate(1)
                py_mm(2)
                py_mm(3)
                gate(2)
                gate(3)
        slab0, sk0 = self.wnext()
        slab1, sk1 = self.wnext()
        wo = [(self.v8(slab0), sk0), (self.v8(slab1), sk1)]
        def wout_tile(tt):
            for half in range(2):
                s8, sk = wo[half]
                hs = slice(half * 512, (half + 1) * 512)
                p, pk = self.psum()
                for fc in range(8):
                    self.mm(p[:, :], self.YT[:, fc, tt * 128:(tt + 1) * 128], s8[:, fc, :], fc == 0, fc == 7,
                            r=[sk, ("YT", fc)], w=[pk])
                self.tt("dve", self.X[:, tt, hs], self.X[:, tt, hs], p[:, :], ALU.add, r=[("X", tt), pk], w=[("X", tt)])
            self.norm_pre(tt)

        self.dbg("YT", self.YT[:, :, :], [("YT", k) for k in range(8)])
        wout_tile(0)
        for tt in range(4):
            if tt + 1 < 4:
                wout_tile(tt + 1)
            self.norm_tr(tt, 8)

    def ffn(self):
        cw = lambda j, cc: self.PAR[:, 36 + j * NCC + cc:37 + j * NCC + cc]
        for cb in range(6):
            ncol = 512 if cb < 5 else 256
            nm = ncol // 128
            slab_a, ska = self.wnext()
            a8 = self.v8(slab_a, ncol)
            slab_v, skv = self.wnext()
            v8_ = self.v8(slab_v, ncol)
            def a_chunk(m):
                cc = cb * 4 + m
                i = cc % 2
                p, pk = self.psum()
                self.fm_proj(p[:, :], pk, a8, ska, m * 128, 128)
                A = self.ACV[i]
                self.cp("pool", A[:, 0:2], self.CARRY[:, cc, :], r=["CARRY"], w=[("ACV", i)])
                self.cp("act", A[:, 2:514], p[:, :], r=[pk], w=[("ACV", i)])
                self.cp("pool", self.CARRY[:, cc, :], A[:, 512:514], r=[("ACV", i)], w=["CARRY"])
                U = self.U[i]
                self.ts("pool", U[:, :], A[:, 2:514], cw(2, cc), self.PAR[:, 102 + cc:103 + cc], ALU.mult, ALU.add,
                        r=[("ACV", i), "PAR"], w=[("U", i)])
                self.stt(U[:, :], A[:, 1:513], cw(1, cc), U[:, :], ALU.mult, ALU.add, r=[("ACV", i), "PAR", ("U", i)], w=[("U", i)])
                self.stt(U[:, :], A[:, 0:512], cw(0, cc), U[:, :], ALU.mult, ALU.add, r=[("ACV", i), "PAR", ("U", i)], w=[("U", i)])
                self.act(self.GT[:, cc, :], U[:, :], AF.Gelu_apprx_tanh, r=[("U", i)], w=[("GT", cc)])

            def v_chunk(m):
                cc = cb * 4 + m
                p, pk = self.psum()
                self.fm_proj(p[:, :], pk, v8_, skv, m * 128, 128)
                self.tt("dve", self.GT[:, cc, :], self.GT[:, cc, :], p[:, :], ALU.mult, r=[("GT", cc), pk], w=[("GT", cc)])

            a_chunk(0)
            for m in range(nm):
                if m + 1 < nm:
                    a_chunk(m + 1)
                v_chunk(m)
        def dmm(tt, cc, s8, sk, ci):
            self.mm(self.PS[tt][:, :], self.GT[:, cc, tt * 128:(tt + 1) * 128], s8[:, ci, :], cc == 0, cc == NCC - 1,
                    r=[sk, ("GT", cc)], w=[("PS", tt)])

        def radd(tt, hs):
            self.tt("dve", self.X[:, tt, hs], self.X[:, tt, hs], self.PS[tt][:, :], ALU.add,
                    r=[("X", tt), ("PS", tt)], w=[("X", tt)])

        for half in range(2):
            hs = slice(half * 512, (half + 1) * 512)
            slabs = []
            for g in range(3):
                if not (half == 1 and g == 2):
                    slab, sk = self.wnext()
                    slabs.append((self.v8(slab), sk))
                if half == 1 and g == 1:
                    slab, sk = self.wnext()
                    slabs.append((self.v8(slab), sk))
                    break
                nch = 8 if g < 2 else 6
                if half == 0 and g == 2:
                    for tt in range(4):
                        for ci in range(nch):
                            dmm(tt, g * 8 + ci, slabs[g][0], slabs[g][1], ci)
                        radd(tt, hs)
                else:
                    for ci in range(nch):
                        for tt in range(4):
                            dmm(tt, g * 8 + ci, slabs[g][0], slabs[g][1], ci)
            if half == 1:
                for tt in range(4):
                    for g in (1, 2):
                        nch = 8 if g < 2 else 6
                        for ci in range(nch):
                            dmm(tt, g * 8 + ci, slabs[g][0], slabs[g][1], ci)
                    radd(tt, hs)

    def final_norm(self, row0, next_row0=None):
        for tt in range(4):
            self.act(self.XS[:, tt, :], self.X[:, tt, :], AF.Square, r=[("X", tt)], w=[("XS", tt), ("ACC", tt)],
                     accum=self.ACC[:, tt:tt + 1])
            self.act(self.ACC2[:, tt:tt + 1], self.ACC[:, tt:tt + 1], AF.Ln, r=[("ACC", tt), "EPSC"], w=[("ACC2", tt)],
                     bias=self.EPSC[:, 0:1], scale=1.0 / D_MODEL)
            self.act(self.RINV[:, tt:tt + 1], self.ACC2[:, tt:tt + 1], AF.Exp, r=[("ACC2", tt)], w=[("RINV", tt)], scale=-0.5)
        for tt in range(4):
            i = self.out_cnt % 2
            self.out_cnt += 1
            self.stt(self.OUTB[i][:, :], self.X[:, tt, :], self.RINV[:, tt:tt + 1], self.GFIN[:, :], ALU.mult, ALU.mult,
                     r=[("X", tt), ("RINV", tt), "GFIN"], w=[("OUTB", i)])
            self.dma("pool", self.y[row0 + tt * 128:row0 + (tt + 1) * 128, :], self.OUTB[i][:, :], r=[("OUTB", i)], w=[],
                     slot=("out", i))
            if next_row0 is not None:
                self.xload(next_row0, tt)
                self.x_preloaded = True


def _colblock(W, c0, ncol):
    return np.ascontiguousarray(W[:, c0:c0 + ncol].reshape(8, 128, ncol).transpose(1, 0, 2)).reshape(128, 8 * ncol)


def pack_slabs(w_in, wg, wf, wo, w_up, w_down):
    sl = np.zeros((NSLAB, 128, 4096), np.float32)
    small = np.concatenate([w_in[:, 1536:1552], w_in[:, 3088:3096]], axis=1)
    sl[0][:, :192] = _colblock(small, 0, 24)
    sl[1] = _colblock(w_in, 0, 512)
    sl[2] = _colblock(w_in, 512, 512)
    sl[3] = _colblock(w_in, 1024, 512)
    sl[4] = _colblock(w_in, 1552, 512)
    sl[5] = _colblock(w_in, 2064, 512)
    sl[6] = _colblock(w_in, 2576, 512)
    wg3 = wg.reshape(4, 128, 1024).transpose(1, 0, 2)
    sl[7][:, :2048] = np.ascontiguousarray(wg3[:, :, 0:512]).reshape(128, 2048)
    sl[8] = _colblock(w_in, 3096, 512)
    sl[9][:, :2048] = np.ascontiguousarray(wg3[:, :, 512:1024]).reshape(128, 2048)
    sl[10] = _colblock(w_in, 3608, 512)
    wf3 = wf.reshape(4, 128, 1024).transpose(1, 0, 2)
    sl[11][:, :2048] = np.ascontiguousarray(wf3[:, :, 0:512]).reshape(128, 2048)
    sl[12] = _colblock(w_in, 4120, 512)
    sl[13][:, :2048] = np.ascontiguousarray(wf3[:, :, 512:1024]).reshape(128, 2048)
    sl[14] = _colblock(w_in, 4632, 512)
    sl[15] = _colblock(wo, 0, 512)
    sl[16] = _colblock(wo, 512, 512)
    for cb in range(6):
        ncol = 512 if cb < 5 else 256
        sl[17 + 2 * cb][:, :8 * ncol] = _colblock(w_up, cb * 512, ncol)
        sl[18 + 2 * cb][:, :8 * ncol] = _colblock(w_up, D_FF + cb * 512, ncol)
    for half in range(2):
        for g in range(3):
            nch = 8 if g < 2 else 6
            blk = w_down[g * 8 * 128:(g * 8 + nch) * 128, half * 512:(half + 1) * 512]
            sl[29 + half * 3 + g][:, :nch * 512] = np.ascontiguousarray(
                blk.reshape(nch, 128, 512).transpose(1, 0, 2)).reshape(128, nch * 512)
    return sl


def make_consts():
    c = np.zeros((128, CW), np.float32)
    c[:, 0:128] = np.eye(128, dtype=np.float32)
    m = np.arange(128)[:, None]
    l = np.arange(128)[None, :]
    c[:, 128:256] = (m <= l).astype(np.float32)
    c[:, 256:384] = np.where(m > l, -30000.0, 0.0)
    c[:, 384:512] = 1.0
    c[64, 512:576] = 1.0
    c[:, 576:1088] = 1.0
    return c


def pack_params(p):
    a = np.zeros((128, PW), np.float32)
    a[:, 0:8] = p["norm_mix_g"][0].reshape(8, 128).T
    a[:, 8:16] = p["norm_ffn_g"][0].reshape(8, 128).T
    a[:, 16:20] = p["gla_out_norm_g"][0].reshape(4, 128).T
    a[:, 20:28] = p["gate_b"][0, 0].reshape(8, 128).T
    a[:, 28:36] = p["gate_b"][0, 1].reshape(8, 128).T
    for j in range(3):
        a[:, 36 + j * NCC:36 + (j + 1) * NCC] = p["conv_w"][0, j].reshape(NCC, 128).T
    a[:, 102:124] = p["conv_b"][0].reshape(NCC, 128).T
    a[:, 124:126] = p["gla_alpha_b"][0].reshape(2, 128).T
    a[0:8, 126] = p["fox_forget_b"][0]
    a[0:64, 127] = p["fox_q_norm_g"][0]
    a[64:128, 127] = p["fox_q_norm_g"][0]
    a[0:64, 128] = p["fox_k_norm_g"][0]
    a[64:128, 128] = p["fox_k_norm_g"][0]
    return a


def run_cores(x, p, ncores, nseq, S, debug=False):
    f = lambda k: np.asarray(p[k], dtype=np.float32)
    pp = {k: f(k) for k in p if k != "x"}
    sl = pack_slabs(pp["w_in"][0], pp["w_gla_branch"][0], pp["w_fox_branch"][0], pp["w_out"][0],
                    pp["w_up"][0], pp["w_down"][0])
    cst = make_consts()
    par = pack_params(pp)
    w2 = np.ascontiguousarray(pp["gla_alpha_w2"][0])
    gfin = np.ascontiguousarray(pp["norm_final_g"])
    bld = Builder(nseq, S)
    bld.debug = debug
    nc = bld.build()
    in_maps = []
    for c in range(ncores):
        xc = np.ascontiguousarray(x[c * nseq:(c + 1) * nseq].reshape(nseq * S, D_MODEL))
        in_maps.append({"x": xc, "wsl": sl, "cst": cst, "par": par, "w2": w2, "gfin": gfin})
    res = run_bass_kernel_spmd(nc, in_maps, core_ids=list(range(ncores)))
    outs = [np.asarray(res.results[c]["y"]).reshape(nseq, S, D_MODEL) for c in range(ncores)]
    out = np.concatenate(outs, axis=0).astype(np.float32)
    if debug:
        return out, {k: np.asarray(v) for k, v in res.results[0].items() if k.startswith("dbg_")}
    return out


def kernel(**inputs):
    x = np.asarray(inputs["x"], dtype=np.float32)
    nseq = BATCH // NCORES
    return run_cores(x, inputs, NCORES, nseq, SEQ)
```

```python
from contextlib import ExitStack
import numpy as np
import concourse.bass as bass
import concourse.mybir as mybir
from concourse.bass_utils import run_bass_kernel_spmd

F32 = mybir.dt.float32
BF16 = mybir.dt.bfloat16
AF = mybir.ActivationFunctionType
ALU = mybir.AluOpType

D_MODEL = 1024
BATCH = 16
SEQ = 4096
NCORES = 8
D_FF = 2816
NCC = 22
EPS = 1e-6
NSLAB = 35
SLAB_USED = {0: 192, 7: 2048, 9: 2048, 11: 2048, 13: 2048, 27: 2048, 28: 2048, 31: 3072, 34: 3072}
ENGS = ("pe", "act", "dve", "pool", "sp")


class Op:
    __slots__ = ("eng", "fn", "r", "w", "slot", "deps", "raw", "signal", "sigidx", "target", "waits")

    def __init__(self, eng, fn, r, w, slot):
        self.eng = eng
        self.fn = fn
        self.r = r
        self.w = w
        self.slot = slot
        self.signal = False
        self.sigidx = 0
        self.target = 0
        self.waits = None


class Prog:
    def __init__(self):
        self.ops = []

    def op(self, eng, fn, r=(), w=()):
        self.ops.append(Op(eng, fn, tuple(r), tuple(w), None))

    def dma(self, eng, fn, r=(), w=(), slot=None):
        assert slot is not None
        self.ops.append(Op(eng, fn, tuple(r), tuple(w), slot))

    def analyze(self):
        last_w = {}
        readers = {}
        slot_cnt = {}
        ops = self.ops
        for i, op in enumerate(ops):
            deps = set()
            for k in op.r:
                j = last_w.get(k)
                if j is not None:
                    deps.add(j)
            op.raw = set(deps)
            for k in op.w:
                j = last_w.get(k)
                if j is not None:
                    deps.add(j)
                rd = readers.get(k)
                if rd:
                    deps.update(rd.values())
            deps.discard(i)
            op.deps = deps
            for k in op.w:
                last_w[k] = i
                readers[k] = {}
            pid = ("dma", op.slot) if op.slot is not None else op.eng
            for k in op.r:
                if k in op.w:
                    continue
                readers.setdefault(k, {})[pid] = i
            if op.slot is not None:
                slot_cnt[op.slot] = slot_cnt.get(op.slot, 0) + 1
                op.target = 16 * slot_cnt[op.slot]
        for op in ops:
            for j in op.deps:
                pj = ops[j]
                if pj.slot is None and (pj.eng != op.eng or op.slot is not None or op.eng != "pe"):
                    pj.signal = True
        cnt = {e: 0 for e in ENGS}
        for op in ops:
            if op.slot is None and op.signal:
                cnt[op.eng] += 1
                op.sigidx = cnt[op.eng]
        seen = {e: {} for e in ENGS}
        for op in ops:
            need = {}
            for j in op.deps:
                pj = ops[j]
                if pj.slot is not None:
                    key = ("dma", pj.slot)
                    val = pj.target
                elif pj.eng == op.eng and op.slot is None and op.eng == "pe":
                    continue
                else:
                    key = pj.eng
                    val = pj.sigidx
                if val > need.get(key, 0):
                    need[key] = val
            waits = []
            sd = seen[op.eng]
            for key, val in need.items():
                if sd.get(key, 0) >= val:
                    continue
                sd[key] = val
                waits.append((key, val))
            op.waits = waits
        self.slot_cnt = slot_cnt
        self.sig_cnt = cnt

    def emit(self, nc, final_slots=()):
        self.analyze()
        with ExitStack() as es:
            sems = {}
            for e in ENGS:
                sems[e] = es.enter_context(nc.semaphore("s_" + e))
            for n, s in enumerate(self.slot_cnt):
                sems[("dma", s)] = es.enter_context(nc.semaphore("d%d" % n))
            block = es.enter_context(nc.Block())

            def run(ename):
                def body(e):
                    for op in self.ops:
                        if op.eng != ename:
                            continue
                        for key, val in op.waits:
                            e.wait_ge(sems[key], val)
                        ins = op.fn(e)
                        if op.slot is not None:
                            ins.then_inc(sems[("dma", op.slot)], 16)
                        elif op.signal:
                            ins.then_inc(sems[ename], 1)
                    if ename == "sp":
                        for s in final_slots:
                            e.wait_ge(sems[("dma", s)], 16 * self.slot_cnt[s])
                return body

            block.tensor(run("pe"))
            block.scalar(run("act"))
            block.vector(run("dve"))
            block.gpsimd(run("pool"))
            block.sync(run("sp"))


CW = 1088
PW = 132


class Builder:
    def __init__(self, nseq, S):
        self.nseq = nseq
        self.S = S
        self.NT = S // 128
        self.NST = S // 512
        self.P = Prog()
        self.use_ar = True
        self.debug = False
        self.dbg_done = False
        self.dbg_slots = []
        self.nc = bass.Bass("TRN2", target_bir_lowering=False)

    def _r(self, r, ar=True):
        r = list(r)
        if ar:
            r.append("AR")
        return r

    def mm(self, out, lhsT, rhs, start, stop, r, w):
        self.P.op("pe", lambda e: e.matmul(out, lhsT=lhsT, rhs=rhs, start=start, stop=stop), self._r(r), w)

    def tr(self, out, in_, ident, r, w):
        self.P.op("pe", lambda e: e.transpose(out=out, in_=in_, identity=ident), self._r(r), w)

    def act(self, out, in_, func, r, w, bias=None, scale=None, accum=None):
        kw = {}
        if bias is not None:
            kw["bias"] = bias
        if scale is not None:
            kw["scale"] = scale
        if accum is not None:
            kw["accum_out"] = accum
        self.P.op("act", lambda e: e.activation(out=out, in_=in_, func=func, **kw), self._r(r), w)

    def tt(self, eng, out, in0, in1, op, r, w):
        self.P.op(eng, lambda e: e.tensor_tensor(out=out, in0=in0, in1=in1, op=op), self._r(r), w)

    def ts(self, eng, out, in0, s1, s2, op0, op1, r, w):
        if s2 is None:
            self.P.op(eng, lambda e: e.tensor_scalar(out=out, in0=in0, scalar1=s1, scalar2=None, op0=op0), self._r(r), w)
        else:
            self.P.op(eng, lambda e: e.tensor_scalar(out=out, in0=in0, scalar1=s1, scalar2=s2, op0=op0, op1=op1), self._r(r), w)

    def stt(self, out, in0, scalar, in1, op0, op1, r, w):
        self.P.op("dve", lambda e: e.scalar_tensor_tensor(out=out, in0=in0, scalar=scalar, in1=in1, op0=op0, op1=op1), self._r(r), w)

    def cp(self, eng, out, in_, r, w):
        if eng == "act":
            self.P.op("act", lambda e: e.activation(out=out, in_=in_, func=AF.Copy), self._r(r), w)
        else:
            self.P.op(eng, lambda e: e.tensor_copy(out=out, in_=in_), self._r(r), w)

    def scan(self, out, data0, data1, initial, r, w):
        self.P.op("dve", lambda e: e.tensor_tensor_scan(out=out, data0=data0, data1=data1, initial=initial,
                                                        op0=ALU.mult, op1=ALU.add), self._r(r), w)

    def recip(self, out, in_, r, w):
        self.P.op("dve", lambda e: e.reciprocal(out=out, in_=in_), self._r(r), w)

    def memset(self, eng, ap, val, w, r=()):
        self.P.op(eng, lambda e: e.memset(ap, val), self._r(r), w)

    def dma(self, eng, out, in_, r, w, slot, ar=True):
        self.P.dma(eng, lambda e: e.dma_start(out=out, in_=in_), self._r(r, ar), w, slot)

    def dbg(self, name, ap, rkeys):
        if not self.debug or self.dbg_done:
            return
        t = self.nc.dram_tensor("dbg_" + name, list(ap.shape), ap.dtype, kind="ExternalOutput").ap()
        self.dma("sp", t, ap, r=rkeys, w=[], slot=("dbg", name), ar=False)
        self.dbg_slots.append(("dbg", name))

    def barrier(self):
        self.P.op("dve", lambda e: e.memset(self.DUMMY[:, 0:1], 0.0), (), ("AR",))

    def psum(self):
        if self.ps_wide:
            i = self.ps_rot6
            self.ps_rot6 = (self.ps_rot6 + 1) % 6
            return self.PSALL[i]
        i = self.ps_rot
        self.ps_rot = (self.ps_rot + 1) % 4
        return self.PS[i], ("PS", i)

    def wnext(self):
        n = self.w_used
        total = self.w_total
        while self.w_issued < min(n + 2, total):
            m = self.w_issued
            slab = m % NSLAB
            slot = m % 3
            wd = SLAB_USED.get(slab, 4096)
            self.dma("sp", self.WS[slot][:, 0:wd], self.wbf[slab][:, 0:wd], r=[("WBF", slab)], w=[("WS", slot)],
                     slot=("w", slot), ar=False)
            self.w_issued += 1
        self.w_used += 1
        slot = n % 3
        return self.WS[slot], ("WS", slot)

    def build(self):
        nc = self.nc
        nseq, S, NT, NST = self.nseq, self.S, self.NT, self.NST
        ntok = nseq * S
        dt = nc.dram_tensor
        self.x = dt("x", [ntok, D_MODEL], F32, kind="ExternalInput").ap()
        self.wsl = dt("wsl", [NSLAB, 128, 4096], F32, kind="ExternalInput").ap()
        self.cst = dt("cst", [128, CW], F32, kind="ExternalInput").ap()
        self.par = dt("par", [128, PW], F32, kind="ExternalInput").ap()
        self.w2 = dt("w2", [16, 256], F32, kind="ExternalInput").ap()
        self.gfin = dt("gfin", [D_MODEL], F32, kind="ExternalInput").ap()
        self.y = dt("y", [ntok, D_MODEL], F32, kind="ExternalOutput").ap()
        self.wbf = dt("wbf", [NSLAB, 128, 4096], BF16, kind="Internal").ap()
        self.kcache = dt("kcache", [8, 68, S], BF16, kind="Internal").ap()
        self.vcache = dt("vcache", [8, 128, NT, 65], BF16, kind="Internal").ap()

        es = ExitStack()
        sb = lambda n, s, d: es.enter_context(nc.sbuf_tensor(n, s, d))
        ps = lambda n, s, d: es.enter_context(nc.psum_tensor(n, s, d))
        self.CST = sb("CST", [128, CW], F32)
        self.CB = sb("CB", [128, 384], BF16)
        self.PAR = sb("PAR", [128, PW], F32)
        self.DER = sb("DER", [128, 8], F32)
        self.GFIN = sb("GFIN", [128, D_MODEL], F32)
        self.W2f = sb("W2f", [16, 256], F32)
        self.W2b = sb("W2b", [16, 256], BF16)
        self.NEGC = sb("NEGC", [128, NT, 8], F32)
        self.X = sb("X", [128, 4, D_MODEL], F32)
        self.XS = sb("XS", [128, 4, D_MODEL], BF16)
        self.XT = sb("XT", [128, 8, 512], BF16)
        self.JUNK = sb("JUNK", [128, 128], BF16)
        self.WS = [sb("WS%d" % i, [128, 4096], BF16) for i in range(3)]
        self.KTN = sb("KTN", [128, 8, 512], BF16)
        self.VTN = sb("VTN", [128, 8, 4, 65], BF16)
        self.KH = [sb("KH%d" % i, [128, S], BF16) for i in range(2)]
        self.VH = [sb("VH%d" % i, [128, NT * 65 + 64], BF16) for i in range(2)]
        self.YT = sb("YT", [128, 8, 512], BF16)
        self.OUTB = [sb("OUTB%d" % i, [128, D_MODEL], F32) for i in range(2)]
        self.ACC = sb("ACC", [128, 8], F32)
        self.ACC2 = sb("ACC2", [128, 8], F32)
        self.RINV = sb("RINV", [128, 8], F32)
        self.SST = sb("SST", [128, 2, 128], F32)
        self.CARRY = sb("CARRY", [128, NCC, 2], F32)
        self.CPC = sb("CPC", [128, 1], F32)
        self.NBL = sb("NBL", [128, 8], F32)
        self.DUMMY = sb("DUMMY", [128, 2], F32)
        self.EPSC = sb("EPSC", [128, 1], F32)
        self.bd_ones = sb("BDONES", [128, 128], BF16)
        NAR = 21696
        self.AR = sb("ARENA", [128, NAR], F32)
        self.PS = [ps("PS%d" % i, [128, 512], F32) for i in range(5)]
        self.PACC = ps("PACC", [128, 512], F32)
        self.PB = [ps("PB%d" % i, [128, 512], BF16) for i in range(2)]
        self.ps_rot = 0
        self.ACCS = [(self.PACC, "PACC"), (self.PS[4], ("PS", 4))]
        self.acc_rot = 0
        self.ps_wide = True
        self.ps_rot6 = 0
        self.PSALL = [(self.PS[i], ("PS", i)) for i in range(5)] + [(self.PACC, "PACC")]

        off = [0]

        def a_f32(n):
            o = off[0]
            off[0] += n
            assert off[0] <= NAR, off[0]
            return self.AR[:, o:o + n]

        def a_bf16(n):
            w = (n + 1) // 2
            o = off[0]
            off[0] += w
            assert off[0] <= NAR, off[0]
            return self.AR[:, o:o + w].bitcast(BF16)

        self.GLRT = a_bf16(512)
        self.LFA = a_f32(512)
        self.LFB = a_f32(512)
        self.CP = a_f32(512)
        self.CNb = a_bf16(512)
        self.SP = a_f32(512)
        self.Bc = a_f32(1024).rearrange("p (c t) -> p c t", c=2)
        self.EQ = a_f32(1024).rearrange("p (c t) -> p c t", c=2)
        self.EK = a_bf16(1024).rearrange("p (c t) -> p c t", c=2)
        self.EKD = a_bf16(1024).rearrange("p (c t) -> p c t", c=2)
        self.QE = a_bf16(1024).rearrange("p (c t) -> p c t", c=2)
        self.KE = a_bf16(1024).rearrange("p (c t) -> p c t", c=2)
        self.KD = a_bf16(1024).rearrange("p (c t) -> p c t", c=2)
        self.VG = a_bf16(2048).rearrange("p (a t) -> p a t", a=4)
        self.SG = a_bf16(2048).rearrange("p (a t) -> p a t", a=4)
        self.SC = [a_bf16(128) for _ in range(2)]
        self.KDT8 = [a_bf16(128) for _ in range(8)]
        self.SSB4 = [a_bf16(256).rearrange("p (c v) -> p c v", c=2) for _ in range(4)]
        self.OGN = [a_bf16(512) for _ in range(2)]
        self.OGT = a_bf16(2048).rearrange("p (a t) -> p a t", a=4)
        self.SQ = [a_bf16(512) for _ in range(3)]
        self.RB = [a_f32(512) for _ in range(2)]
        self.QT = a_bf16(4096).rearrange("p (h t) -> p h t", h=8)
        self.PT = [a_bf16(512) for _ in range(3)]
        self.OS = a_f32(512)
        self.LI = a_f32(512)
        self.OFT = a_bf16(2048).rearrange("p (c t) -> p c t", c=4)
        self.OSH = [a_bf16(512) for _ in range(2)]
        self.SGG = [a_f32(512) for _ in range(2)]
        self.MT = [a_f32(512) for _ in range(2)]
        a_end = off[0]
        off[0] = 0
        self.GT = a_bf16(NCC * 512).rearrange("p (c t) -> p c t", c=NCC)
        self.ACV = [a_f32(516) for _ in range(2)]
        self.U = [a_f32(512) for _ in range(2)]
        off[0] = 0
        self.STG = [a_f32(4096) for _ in range(2)]

        self.ident_f = self.CST[:, 0:128]
        self.trimask = self.CST[:, 128:256]
        self.ones_f = self.CST[:, 384:512]
        self.sel65 = self.CST[:, 512:576]
        self.ones512 = self.CST[:, 576:1088]
        self.kv_cnt = 0
        self.x_preloaded = False
        self.fox_pending = None
        self.ident_b = self.CB[:, 0:128]
        self.causal_b = self.CB[:, 128:256]
        self.ones_b = self.CB[:, 256:384]

        self.prologue()
        self.w_used = 0
        self.w_issued = 0
        self.w_total = NSLAB * nseq * NST
        self.out_cnt = 0
        for q in range(nseq):
            self.seq_init()
            for st in range(NST):
                self.supertile(q, st)
        self.P.emit(nc, final_slots=[("out", 0), ("out", 1)] + self.dbg_slots)
        es.close()
        return nc

    def prologue(self):
        self.dma("pool", self.CST[:, :], self.cst[:, :], r=[], w=["CST"], slot="lcst")
        self.dma("pool", self.PAR[:, :], self.par[:, :], r=[], w=["PAR"], slot="lpar")
        self.dma("pool", self.W2f[0:16, :], self.w2[:, :], r=[], w=["W2f"], slot="lw2")
        self.dma("pool", self.GFIN[:, :], self.gfin.partition_broadcast(128), r=[], w=["GFIN"], slot="lgf")
        self.cp("dve", self.CB[:, 0:128], self.CST[:, 0:128], r=["CST"], w=["CB"])
        self.cp("dve", self.CB[:, 128:256], self.CST[:, 256:384], r=["CST"], w=["CB"])
        self.cp("dve", self.CB[:, 256:384], self.CST[:, 384:512], r=["CST"], w=["CB"])
        self.cp("dve", self.W2b[0:16, :], self.W2f[0:16, :], r=["W2f"], w=["W2b"])
        self.ts("dve", self.DER[:, 0:2], self.PAR[:, 124:126], -1.0, None, ALU.mult, None, r=["PAR"], w=["DER"])
        self.ts("dve", self.DER[0:8, 2:3], self.PAR[0:8, 126:127], -1.0, None, ALU.mult, None, r=["PAR"], w=["DER"])
        self.ts("dve", self.DER[:, 3:4], self.PAR[:, 127:128], 0.125, None, ALU.mult, None, r=["PAR"], w=["DER"])
        self.memset("dve", self.EPSC[:, :], EPS, w=["EPSC"])
        self.memset("dve", self.bd_ones[:, :], 0.0, w=["BD"])
        self.memset("dve", self.bd_ones[0:64, 0:64], 1.0, w=["BD"])
        self.memset("dve", self.bd_ones[64:128, 64:128], 1.0, w=["BD"])
        for n in range(NSLAB):
            s2 = n % 2
            s3 = n % 3
            wd = SLAB_USED.get(n, 4096)
            self.dma("sp", self.STG[s2][:, 0:wd], self.wsl[n][:, 0:wd], r=[], w=[("STG", s2)], slot=("pin", s2))
            eng = "act" if n % 2 == 0 else "dve"
            self.cp(eng, self.WS[s3][:, 0:wd], self.STG[s2][:, 0:wd], r=[("STG", s2)], w=[("WS", s3)])
            self.dma("pool", self.wbf[n][:, 0:wd], self.WS[s3][:, 0:wd], r=[("WS", s3)], w=[("WBF", n)], slot=("pout", s3))
        self.barrier()
        for h in range(8):
            self.dma("pool", self.QT[65:68, h, :], self.ones512[0:3, :], r=["CST"], w=["QT1"], slot=("q1", h))
        self.memset("dve", self.KTN[64:65, :, :], 1.0, w=["KTN1"])
        self.memset("dve", self.VTN[:, :, :, 64:65], 1.0, w=["VTN1"])
        for b_ in range(2):
            self.memset("pool", self.VH[b_][:, :], 0.0, w=[("VH", b_)])

    def seq_init(self):
        self.memset("dve", self.SST[:, :, :], 0.0, w=["SST"])
        self.memset("dve", self.CARRY[:, :, :], 0.0, w=["CARRY"])
        self.memset("dve", self.CPC[:, :], 0.0, w=["CPC"])

    def rinv4(self, n_inv, ncols):
        k4 = lambda n: [n] + [(n, t) for t in range(4)]
        self.act(self.ACC2[:, 0:ncols], self.ACC[:, 0:ncols], AF.Ln, r=k4("ACC") + ["EPSC"], w=k4("ACC2"), bias=self.EPSC[:, 0:1], scale=n_inv)
        self.act(self.RINV[:, 0:ncols], self.ACC2[:, 0:ncols], AF.Exp, r=k4("ACC2"), w=k4("RINV"), scale=-0.5)

    def norm_pre(self, tt):
        self.act(self.XS[:, tt, :], self.X[:, tt, :], AF.Square, r=[("X", tt)], w=[("XS", tt), ("ACC", tt)],
                 accum=self.ACC[:, tt:tt + 1])
        self.act(self.ACC2[:, tt:tt + 1], self.ACC[:, tt:tt + 1], AF.Ln, r=[("ACC", tt), "EPSC"], w=[("ACC2", tt)],
                 bias=self.EPSC[:, 0:1], scale=1.0 / D_MODEL)
        self.act(self.RINV[:, tt:tt + 1], self.ACC2[:, tt:tt + 1], AF.Exp, r=[("ACC2", tt)], w=[("RINV", tt)], scale=-0.5)
        self.ts("dve", self.XS[:, tt, :], self.X[:, tt, :], self.RINV[:, tt:tt + 1], None, ALU.mult, None,
                r=[("X", tt), ("RINV", tt)], w=[("XS", tt)])

    def norm_tr(self, tt, gcol):
        for g4 in range(2):
            pb = self.PB[g4]
            pk = ("PB", g4)
            for k in range(4):
                kc = g4 * 4 + k
                self.tr(pb[:, k * 128:(k + 1) * 128], self.XS[:, tt, kc * 128:(kc + 1) * 128], self.ident_b,
                        r=[("XS", tt), "CB"], w=[pk])
            gain = self.PAR[:, gcol + g4 * 4:gcol + g4 * 4 + 4].unsqueeze(2).to_broadcast([128, 4, 128])
            self.tt("dve", self.XT[:, g4 * 4:(g4 + 1) * 4, tt * 128:(tt + 1) * 128],
                    pb[:, :].rearrange("p (a t) -> p a t", a=4), gain, ALU.mult,
                    r=[pk, "PAR"], w=[("XT", kc_) for kc_ in range(g4 * 4, g4 * 4 + 4)])

    def norm_tile(self, tt, gcol):
        self.norm_pre(tt)
        self.norm_tr(tt, gcol)

    def fm_proj(self, out, okey, slab8, skey, col0, M):
        xr = [("XT", kc) for kc in range(8)]
        for kc in range(8):
            self.mm(out, slab8[:, kc, col0:col0 + M], self.XT[:, kc, :], kc == 0, kc == 7,
                    r=[skey, ("XT", kc)], w=[okey])

    def tm_proj(self, out, okey, slab8, skey, col0, N, tt):
        for kc in range(8):
            self.mm(out, self.XT[:, kc, tt * 128:(tt + 1) * 128], slab8[:, kc, col0:col0 + N], kc == 0, kc == 7,
                    r=[skey, ("XT", kc)], w=[okey])

    @staticmethod
    def v8(slab, ncol=512):
        return slab[:, 0:8 * ncol].rearrange("p (k c) -> p k c", k=8)

    def xload(self, row0, tt):
        self.dma("pool", self.X[:, tt, :], self.x[row0 + tt * 128:row0 + (tt + 1) * 128, :],
                 r=[], w=[("X", tt)], slot=("lx", tt), ar=False)

    def supertile(self, q, st):
        S, NT = self.S, self.NT
        row0 = q * S + st * 512
        if not self.x_preloaded:
            for tt in range(4):
                self.xload(row0, tt)
        self.x_preloaded = False
        for tt in range(4):
            self.norm_tile(tt, 0)
        self.dbg("XT", self.XT[:, :, :], [("XT", k) for k in range(8)])

        slab, sk = self.wnext()
        s8 = self.v8(slab, 24)
        p, pk = self.psum()
        self.fm_proj(p[0:16, :], pk, s8, sk, 0, 16)
        self.cp("act", self.GLRT[0:16, :], p[0:16, :], r=[pk], w=["GLRT"])
        p, pk = self.psum()
        self.fm_proj(p[0:8, :], pk, s8, sk, 16, 8)
        self.act(self.LFA[0:8, :], p[0:8, :], AF.Exp, r=[pk, "DER"], w=["LFA"], bias=self.DER[0:8, 2:3], scale=-1.0)
        self.act(self.LFB[0:8, :], self.LFA[0:8, :], AF.Ln, r=["LFA"], w=["LFB"], bias=1.0)
        self.scan(self.CP[0:8, :], self.ones512[0:8, :], self.LFB[0:8, :], self.CPC[0:8, 0:1],
                  r=["LFB", "CPC", "CST"], w=["CP"])
        self.cp("dve", self.CPC[0:8, 0:1], self.CP[0:8, 511:512], r=["CP"], w=["CPC"])
        self.ts("dve", self.CNb[0:8, :], self.CP[0:8, :], -1.0, None, ALU.mult, None, r=["CP"], w=["CNb"])
        for h in range(8):
            self.dma("pool", self.QT[64:65, h, :], self.CNb[h:h + 1, :], r=["CNb"], w=[("QTc", h)], slot=("crow", h))
        S1, S2, S3 = self.SQ[0], self.SQ[1], self.SQ[2]
        self.cp("dve", S1[0:8, :], self.CP[0:8, :], r=["CP"], w=[("SQ", 0)])
        self.tt("dve", self.LFA[0:8, :], self.CP[0:8, :], S1[0:8, :], ALU.subtract, r=["CP", ("SQ", 0)], w=["LFA"])
        self.cp("dve", S2[0:8, :], self.LFA[0:8, :], r=["LFA"], w=[("SQ", 1)])
        self.tt("dve", self.LFB[0:8, :], self.LFA[0:8, :], S2[0:8, :], ALU.subtract, r=["LFA", ("SQ", 1)], w=["LFB"])
        self.cp("dve", S3[0:8, :], self.LFB[0:8, :], r=["LFB"], w=[("SQ", 2)])
        for h in range(8):
            for i_, Si in enumerate((S1, S2, S3)):
                self.dma("pool", self.KTN[65 + i_:66 + i_, h, :], Si[h:h + 1, :], r=[("SQ", i_)], w=[("KTNc", h, i_)],
                         slot=("krow", h))

        slab, sk = self.wnext()
        s8 = self.v8(slab)
        for c in range(2):
            p, pk = self.psum()
            self.mm(p[:, :], self.W2b[0:16, c * 128:(c + 1) * 128], self.GLRT[0:16, :], True, True,
                    r=["W2b", "GLRT"], w=[pk])
            self.act(self.SP[:, :], p[:, :], AF.Exp, r=[pk, "DER"], w=["SP"], bias=self.DER[:, c:c + 1], scale=-1.0)
            self.act(self.SP[:, :], self.SP[:, :], AF.Ln, r=["SP"], w=["SP"], bias=1.0)
            for tt in range(4):
                self.scan(self.Bc[:, c, tt * 128:(tt + 1) * 128], self.ones_f[:, :], self.SP[:, tt * 128:(tt + 1) * 128],
                          0.0, r=["SP", "CST"], w=[("Bc", c)])
            self.act(self.EQ[:, c, :], self.Bc[:, c, :], AF.Exp, r=[("Bc", c)], w=[("EQ", c)], scale=-1.0 / 16)
            self.act(self.EK[:, c, :], self.Bc[:, c, :], AF.Exp, r=[("Bc", c)], w=[("EK", c)], scale=1.0 / 16)
            self.ts("dve", self.NBL[:, c * 4:(c + 1) * 4], self.Bc[:, c, 127::128], -1.0 / 16, None, ALU.mult, None,
                    r=[("Bc", c)], w=["NBL"])
            for tt in range(4):
                self.act(self.EKD[:, c, tt * 128:(tt + 1) * 128], self.Bc[:, c, tt * 128:(tt + 1) * 128], AF.Exp,
                         r=[("Bc", c), "NBL"], w=[("EKD", c)], bias=self.NBL[:, c * 4 + tt:c * 4 + tt + 1], scale=1.0 / 16)
            p, pk = self.psum()
            self.fm_proj(p[:, :], pk, s8, sk, c * 128, 128)
            self.stt(self.QE[:, c, :], p[:, :], 0.125, self.EQ[:, c, :], ALU.mult, ALU.mult, r=[pk, ("EQ", c)], w=[("QE", c)])
            p, pk = self.psum()
            self.fm_proj(p[:, :], pk, s8, sk, 256 + c * 128, 128)
            self.tt("dve", self.KE[:, c, :], p[:, :], self.EK[:, c, :], ALU.mult, r=[pk, ("EK", c)], w=[("KE", c)])
            self.tt("dve", self.KD[:, c, :], p[:, :], self.EKD[:, c, :], ALU.mult, r=[pk, ("EKD", c)], w=[("KD", c)])
        slab, sk = self.wnext()
        s8 = self.v8(slab)
        for tt in range(4):
            p, pk = self.psum()
            self.tm_proj(p[:, :], pk, s8, sk, 0, 512, tt)
            self.cp("act", self.VG[:, tt, :], p[:, :], r=[pk], w=[("VG", tt)])
        self.gla_prep()
        slab, sk = self.wnext()
        s8 = self.v8(slab)
        for tt in range(4):
            p, pk = self.psum()
            self.tm_proj(p[:, :], pk, s8, sk, 0, 512, tt)
            self.act(self.SG[:, tt, :], p[:, :], AF.Silu, r=[pk], w=[("SG", tt)])

        self.dbg("QE", self.QE[:, :, :], [("QE", 0), ("QE", 1)])
        self.dbg("KE", self.KE[:, :, :], [("KE", 0), ("KE", 1)])
        self.dbg("KD", self.KD[:, :, :], [("KD", 0), ("KD", 1)])
        self.dbg("VG", self.VG[:, :, :], [("VG", t) for t in range(4)])
        self.dbg("SG", self.SG[:, :, :], [("SG", t) for t in range(4)])
        self.dbg("CP", self.CP[0:8, :], ["CP"])

        self.ps_wide = True
        for which in range(2):
            slab, sk = self.wnext()
            s8 = self.v8(slab)
            gvec = self.DER[:, 3:4] if which == 0 else self.PAR[:, 128:129]
            gkey = "DER" if which == 0 else "PAR"

            def dst(h):
                if which == 0:
                    return self.QT[0:64, h, :], ("QT", h)
                return self.KTN[0:64, h, :], ("KTN", h)

            def stage_a(pr):
                p, pk = self.psum()
                self.fm_proj(p[:, :], pk, s8, sk, pr * 128, 128)
                self.act(self.SQ[pr % 3][:, :], p[:, :], AF.Square, r=[pk], w=[("SQ", pr % 3)])
                self.ts("dve", self.PT[pr % 3][:, :], p[:, :], gvec, None, ALU.mult, None,
                        r=[pk, gkey, ("SQ", pr % 3)], w=[("PT", pr % 3)])

            def stage_b(pr):
                i = pr % 2
                p2, pk2 = self.psum()
                self.mm(p2[:, :], self.bd_ones[:, :], self.SQ[pr % 3][:, :], True, True, r=["BD", ("SQ", pr % 3)], w=[pk2])
                self.act(self.RB[i][:, :], p2[:, :], AF.Ln, r=[pk2, "EPSC"], w=[("RB", i)], bias=self.EPSC[:, 0:1], scale=1.0 / 64)
                self.act(self.RB[i][:, :], self.RB[i][:, :], AF.Exp, r=[("RB", i)], w=[("RB", i)], scale=-0.5)
                d0, dk0 = dst(2 * pr)
                d1, dk1 = dst(2 * pr + 1)
                j3 = pr % 3
                self.tt("dve", d0, self.PT[j3][0:64, :], self.RB[i][0:64, :], ALU.mult, r=[("PT", j3), ("RB", i)], w=[dk0])
                self.tt("dve", self.PT[j3][64:128, :], self.PT[j3][64:128, :], self.RB[i][64:128, :], ALU.mult,
                        r=[("PT", j3), ("RB", i)], w=[("PT", j3)])
                self.dma("pool", d1, self.PT[j3][64:128, :], r=[("PT", j3)], w=[dk1], slot=("hshift", which * 4 + pr))

            stage_a(0)
            stage_a(1)
            for pr in range(4):
                if pr + 2 < 4:
                    stage_a(pr + 2)
                stage_b(pr)
        self.ps_wide = True
        self.dma("pool", self.kcache[:, :, st * 512:(st + 1) * 512].rearrange("h d s -> d h s"), self.KTN[0:68, :, :],
                 r=[("KTN", h) for h in range(8)] + ["KTN1"] + [("KTNc", h, i_) for h in range(8) for i_ in range(3)],
                 w=[("KDR", st)], slot="kst")
        slab, sk = self.wnext()
        s8 = self.v8(slab)
        for tt in range(4):
            p, pk = self.psum()
            self.tm_proj(p[:, :], pk, s8, sk, 0, 512, tt)
            self.cp("act", self.VTN[:, :, tt, 0:64], p[:, :].rearrange("p (h d) -> p h d", h=8), r=[pk], w=[("VTN", tt)])
        self.dma("pool", self.vcache[:, :, st * 4:(st + 1) * 4, :].rearrange("h p j d -> p h (j d)"),
                 self.VTN[:, :, :, :].rearrange("p h j d -> p h (j d)"),
                 r=[("VTN", t) for t in range(4)] + ["VTN1"], w=[("VDR", st)], slot="vst")

        self.dbg("QT", self.QT[0:65, :, :], [("QT", h) for h in range(8)] + [("QTc", h) for h in range(8)])
        self.dbg("KTN", self.KTN[0:65, :, :], [("KTN", h) for h in range(8)] + ["KTN1"])
        self.dbg("VTN", self.VTN[:, :, :, :], [("VTN", t) for t in range(4)] + ["VTN1"])
        self.fox_first = self.kv_load(st, 0)
        self.gla_tiles()
        self.dbg("OGT", self.OGT[:, :, :], ["OGT"])
        self.fox(st)
        self.dbg("OFT", self.OFT[:, :, :], [("OFT", h) for h in range(8)])

        self.merge()

        self.dbg("X1", self.X[:, :, :], [("X", t) for t in range(4)])
        self.barrier()
        self.ffn()
        self.dbg("GT", self.GT[:, :, :], [("GT", c) for c in range(NCC)])
        self.dbg("X2", self.X[:, :, :], [("X", t) for t in range(4)])
        nrow = row0 + 512 if row0 + 512 < self.nseq * self.S else None
        self.final_norm(row0, nrow)
        self.barrier()
        self.dbg_done = True

    def kv_load(self, st, h):
        b = self.kv_cnt % 2
        self.kv_cnt += 1
        nk = (st + 1) * 512
        nb = (st + 1) * 4
        deps_k = [("KDR", s) for s in range(st + 1)]
        deps_v = [("VDR", s) for s in range(st + 1)]
        self.dma("sp", self.KH[b][0:68, 0:nk], self.kcache[h, :, 0:nk], r=deps_k, w=[("KH", b)], slot=("kh", b), ar=False)
        self.dma("sp", self.VH[b][:, 0:nb * 65], self.vcache[h, :, 0:nb, :].rearrange("p j d -> p (j d)"), r=deps_v, w=[("VH", b)], slot=("vh", b), ar=False)
        return b

    def gla_prep(self):
        for tt in range(4):
            ts_ = slice(tt * 128, (tt + 1) * 128)
            for c in range(2):
                idx = tt * 2 + c
                pb = self.PB[idx % 2]
                pbk = ("PB", idx % 2)
                self.tr(pb[:, 0:128], self.KD[:, c, ts_], self.ident_b, r=[("KD", c), "CB"], w=[pbk])
                self.cp("act" if idx % 2 == 0 else "dve", self.KDT8[idx][:, :], pb[:, 0:128], r=[pbk], w=[("KDT", idx)])
        for tt in range(4):
            for c in range(2):
                idx = tt * 2 + c
                self.mm(self.PS[tt][:, c * 256:(c + 1) * 256], self.KDT8[idx][:, :], self.VG[:, tt, c * 256:(c + 1) * 256],
                        True, True, r=[("KDT", idx), ("VG", tt)], w=[("PS", tt)])
        for tt in range(4):
            self.cp("dve", self.SSB4[tt][:, :, :], self.SST[:, :, :], r=["SST"], w=[("SSB", tt)])
            col = tt * 128 + 127
            for c in range(2):
                for hp in range(2):
                    r0 = hp * 64
                    self.stt(self.SST[r0:r0 + 64, c, :], self.SST[r0:r0 + 64, c, :], self.EQ[r0:r0 + 64, c, col:col + 1],
                             self.PS[tt][r0:r0 + 64, c * 256 + hp * 128:c * 256 + (hp + 1) * 128], ALU.mult, ALU.add,
                             r=["SST", ("EQ", c), ("PS", tt)], w=["SST"])
    def gla_tiles(self):
        self.ps_wide = False
        def tile_mm(tt):
            ts_ = slice(tt * 128, (tt + 1) * 128)
            acc, acck = self.ACCS[self.acc_rot % 2]
            self.acc_rot += 1

            def sc(h):
                c = h // 2
                b0 = (h % 2) * 64
                p, pk = self.psum()
                self.mm(p[:, 0:128], self.KE[b0:b0 + 64, c, ts_], self.QE[b0:b0 + 64, c, ts_], True, True,
                        r=[("KE", c), ("QE", c)], w=[pk])
                self.tt("dve", self.SC[h % 2][:, :], p[:, 0:128], self.trimask, ALU.mult, r=[pk, "CST"], w=[("SC", h % 2)])

            def oo(h):
                c = h // 2
                b0 = (h % 2) * 64
                hs = slice(h * 128, (h + 1) * 128)
                self.mm(acc[:, hs], self.SC[h % 2][:, :], self.VG[:, tt, hs], True, False,
                        r=[("SC", h % 2), ("VG", tt)], w=[acck])
                self.mm(acc[:, hs], self.QE[b0:b0 + 64, c, ts_], self.SSB4[tt][b0:b0 + 64, c, :], False, True,
                        r=[("QE", c), ("SSB", tt)], w=[acck])

            sc(0)
            for h in range(4):
                if h + 1 < 4:
                    sc(h + 1)
                oo(h)
            return acc, acck

        def tile_post(tt, acc, acck):
            ts_ = slice(tt * 128, (tt + 1) * 128)
            for h in range(4):
                self.act(self.OGN[tt % 2][:, h * 128:(h + 1) * 128], acc[:, h * 128:(h + 1) * 128], AF.Square, r=[acck],
                         w=[("OGN", tt % 2), "ACC"] + [("ACC", t) for t in range(4)], accum=self.ACC[:, h:h + 1])
            self.rinv4(1.0 / 128, 4)
            for h in range(4):
                hs = slice(h * 128, (h + 1) * 128)
                self.stt(self.OGN[tt % 2][:, hs], acc[:, hs], self.RINV[:, h:h + 1], self.SG[:, tt, hs], ALU.mult, ALU.mult,
                         r=[acck, "RINV", ("SG", tt)], w=[("OGN", tt % 2)])

        def tile_tr(tt):
            ts_ = slice(tt * 128, (tt + 1) * 128)
            pb = self.PB[tt % 2]
            pbk = ("PB", tt % 2)
            for c4 in range(4):
                self.tr(pb[:, c4 * 128:(c4 + 1) * 128], self.OGN[tt % 2][:, c4 * 128:(c4 + 1) * 128], self.ident_b,
                        r=[("OGN", tt % 2), "CB"], w=[pbk])
            self.tt("dve", self.OGT[:, :, ts_], pb[:, :].rearrange("p (a t) -> p a t", a=4),
                    self.PAR[:, 16:20].unsqueeze(2).to_broadcast([128, 4, 128]), ALU.mult,
                    r=[pbk, "PAR"], w=["OGT"])

        cur = tile_mm(0)
        for tt in range(4):
            nxt = tile_mm(tt + 1) if tt + 1 < 4 else None
            tile_post(tt, *cur)
            tile_tr(tt)
            cur = nxt
        self.ps_wide = True

    def fox(self, st):
        nblk = (st + 1) * 4
        self.ps_wide = False
        nxt = self.fox_first
        for h in range(8):
            b = nxt
            if h + 1 < 8:
                nxt = self.kv_load(st, h + 1)
            KH, VH = self.KH[b], self.VH[b]
            qr = [("QT", h), ("QTc", h), "QT1"]
            acc, acck = self.ACCS[self.acc_rot % 2]
            self.acc_rot += 1

            def qk(j):
                diag = j >= 4 * st
                q0 = (j - 4 * st) * 128 if diag else 0
                p, pk = self.psum()
                self.mm(p[:, q0:512], KH[0:68, j * 128:(j + 1) * 128], self.QT[0:68, h, q0:512], True, not diag,
                        r=[("KH", b)] + qr, w=[pk])
                if diag:
                    self.mm(p[:, q0:q0 + 128], self.ident_b, self.causal_b, False, True, r=["CB"], w=[pk])
                return p, pk, q0

            def pv(j, p, pk, q0):
                i = j % 3
                self.act(self.PT[i][:, q0:512], p[:, q0:512], AF.Exp, r=[pk], w=[("PT", i)])
                self.mm(acc[:, q0:512], VH[:, j * 65:j * 65 + 128], self.PT[i][:, q0:512], j == 0, j == nblk - 1,
                        r=[("VH", b), ("PT", i)], w=[acck])

            def normalize(h_, acc_, acck_):
                self.cp("act", self.OS[0:65, :], acc_[0:65, :], r=[acck_], w=["OS"])
                p_, pk_ = self.psum()
                self.mm(p_[0:64, :], self.sel65[0:65, 0:64], self.OS[0:65, :], True, True, r=["CST", "OS"], w=[pk_])
                self.recip(self.LI[0:64, :], p_[0:64, :], r=[pk_], w=["LI"])
                if h_ % 2 == 0:
                    self.tt("pool", self.OFT[0:64, h_ // 2, :], self.OS[0:64, :], self.LI[0:64, :], ALU.mult,
                            r=["OS", "LI"], w=[("OFT", h_)])
                else:
                    k_ = (h_ // 2) % 2
                    self.tt("pool", self.OSH[k_][0:64, :], self.OS[0:64, :], self.LI[0:64, :], ALU.mult,
                            r=["OS", "LI"], w=[("OSH", k_)])
                    self.dma("pool", self.OFT[64:128, h_ // 2, :], self.OSH[k_][0:64, :], r=[("OSH", k_)], w=[("OFT", h_)],
                             slot=("oshift", h_ // 2))

            DEPTH = 2
            pend = [qk(j) for j in range(min(DEPTH, nblk))]
            if self.fox_pending is not None:
                normalize(*self.fox_pending)
            for j in range(nblk):
                if j + DEPTH < nblk:
                    pend.append(qk(j + DEPTH))
                pv(j, *pend.pop(0))
            self.fox_pending = (h, acc, acck)
        normalize(*self.fox_pending)
        self.fox_pending = None
        self.ps_wide = True

    def merge(self):
        for which in range(2):
            for grp in range(2):
                wslab, wk = self.wnext()
                gslab, gsk = self.wnext()
                gs8 = self.v8(gslab)
                W = wslab[:, 0:2048].rearrange("p (k c) -> p k c", k=4)
                pys = {}

                def py_mm(f):
                    py, pyk = self.psum()
                    if which == 0:
                        for c4 in range(4):
                            self.mm(py[:, :], W[:, c4, f * 128:(f + 1) * 128], self.OGT[:, c4, :], c4 == 0, c4 == 3,
                                    r=[wk, "OGT"], w=[pyk])
                    else:
                        for c4 in range(4):
                            self.mm(py[:, :], W[:, c4, f * 128:(f + 1) * 128], self.OFT[:, c4, :], c4 == 0, c4 == 3,
                                    r=[wk, ("OFT", 2 * c4), ("OFT", 2 * c4 + 1)], w=[pyk])
                    pys[f] = (py, pyk)

                def gate(f):
                    fc = grp * 4 + f
                    i = fc % 2
                    py, pyk = pys.pop(f)
                    pg, pgk = self.psum()
                    self.fm_proj(pg[:, :], pgk, gs8, gsk, f * 128, 128)
                    bcol = (20 if which == 0 else 28) + fc
                    self.act(self.SGG[i][:, :], pg[:, :], AF.Sigmoid, r=[pgk, "PAR"], w=[("SGG", i)], bias=self.PAR[:, bcol:bcol + 1])
                    if which == 0:
                        self.tt("dve", self.YT[:, fc, :], py[:, :], self.SGG[i][:, :], ALU.mult, r=[pyk, ("SGG", i)], w=[("YT", fc)])
                    else:
                        self.tt("dve", self.MT[i][:, :], py[:, :], self.SGG[i][:, :], ALU.mult, r=[pyk, ("SGG", i)], w=[("MT", i)])
                        self.tt("pool", self.YT[:, fc, :], self.YT[:, fc, :], self.MT[i][:, :], ALU.add,
                                r=[("YT", fc), ("MT", i)], w=[("YT", fc)])

                py_mm(0)
                py_mm(1)
                gate(0)
                gate(1)
                py_mm(2)
                py_mm(3)
                gate(2)
                gate(3)
        slab0, sk0 = self.wnext()
        slab1, sk1 = self.wnext()
        wo = [(self.v8(slab0), sk0), (self.v8(slab1), sk1)]
        def wout_tile(tt):
            for half in range(2):
                s8, sk = wo[half]
                hs = slice(half * 512, (half + 1) * 512)
                p, pk = self.psum()
                for fc in range(8):
                    self.mm(p[:, :], self.YT[:, fc, tt * 128:(tt + 1) * 128], s8[:, fc, :], fc == 0, fc == 7,
                            r=[sk, ("YT", fc)], w=[pk])
                self.tt("dve", self.X[:, tt, hs], self.X[:, tt, hs], p[:, :], ALU.add, r=[("X", tt), pk], w=[("X", tt)])
            self.norm_pre(tt)

        self.dbg("YT", self.YT[:, :, :], [("YT", k) for k in range(8)])
        wout_tile(0)
        for tt in range(4):
            if tt + 1 < 4:
                wout_tile(tt + 1)
            self.norm_tr(tt, 8)

    def ffn(self):
        cw = lambda j, cc: self.PAR[:, 36 + j * NCC + cc:37 + j * NCC + cc]
        for cb in range(6):
            ncol = 512 if cb < 5 else 256
            nm = ncol // 128
            slab_a, ska = self.wnext()
            a8 = self.v8(slab_a, ncol)
            slab_v, skv = self.wnext()
            v8_ = self.v8(slab_v, ncol)
            def a_chunk(m):
                cc = cb * 4 + m
                i = cc % 2
                p, pk = self.psum()
                self.fm_proj(p[:, :], pk, a8, ska, m * 128, 128)
                A = self.ACV[i]
                self.cp("pool", A[:, 0:2], self.CARRY[:, cc, :], r=["CARRY"], w=[("ACV", i)])
                self.cp("act", A[:, 2:514], p[:, :], r=[pk], w=[("ACV", i)])
                self.cp("pool", self.CARRY[:, cc, :], A[:, 512:514], r=[("ACV", i)], w=["CARRY"])
                U = self.U[i]
                self.ts("pool", U[:, :], A[:, 2:514], cw(2, cc), self.PAR[:, 102 + cc:103 + cc], ALU.mult, ALU.add,
                        r=[("ACV", i), "PAR"], w=[("U", i)])
                self.stt(U[:, :], A[:, 1:513], cw(1, cc), U[:, :], ALU.mult, ALU.add, r=[("ACV", i), "PAR", ("U", i)], w=[("U", i)])
                self.stt(U[:, :], A[:, 0:512], cw(0, cc), U[:, :], ALU.mult, ALU.add, r=[("ACV", i), "PAR", ("U", i)], w=[("U", i)])
                self.act(self.GT[:, cc, :], U[:, :], AF.Gelu_apprx_tanh, r=[("U", i)], w=[("GT", cc)])

            def v_chunk(m):
                cc = cb * 4 + m
                p, pk = self.psum()
                self.fm_proj(p[:, :], pk, v8_, skv, m * 128, 128)
                self.tt("dve", self.GT[:, cc, :], self.GT[:, cc, :], p[:, :], ALU.mult, r=[("GT", cc), pk], w=[("GT", cc)])

            a_chunk(0)
            for m in range(nm):
                if m + 1 < nm:
                    a_chunk(m + 1)
                v_chunk(m)
        for half in range(2):
            hs = slice(half * 512, (half + 1) * 512)
            for g in range(3):
                nch = 8 if g < 2 else 6
                slab, sk = self.wnext()
                s8 = self.v8(slab)
                if g < 2:
                    for ci in range(nch):
                        cc = g * 8 + ci
                        for tt in range(4):
                            self.mm(self.PS[tt][:, :], self.GT[:, cc, tt * 128:(tt + 1) * 128], s8[:, ci, :], cc == 0, cc == NCC - 1,
                                    r=[sk, ("GT", cc)], w=[("PS", tt)])
                else:
                    for tt in range(4):
                        for ci in range(nch):
                            cc = g * 8 + ci
                            self.mm(self.PS[tt][:, :], self.GT[:, cc, tt * 128:(tt + 1) * 128], s8[:, ci, :], cc == 0, cc == NCC - 1,
                                    r=[sk, ("GT", cc)], w=[("PS", tt)])
                        self.tt("dve", self.X[:, tt, hs], self.X[:, tt, hs], self.PS[tt][:, :], ALU.add,
                                r=[("X", tt), ("PS", tt)], w=[("X", tt)])

    def final_norm(self, row0, next_row0=None):
        for tt in range(4):
            self.act(self.XS[:, tt, :], self.X[:, tt, :], AF.Square, r=[("X", tt)], w=[("XS", tt), ("ACC", tt)],
                     accum=self.ACC[:, tt:tt + 1])
            self.act(self.ACC2[:, tt:tt + 1], self.ACC[:, tt:tt + 1], AF.Ln, r=[("ACC", tt), "EPSC"], w=[("ACC2", tt)],
                     bias=self.EPSC[:, 0:1], scale=1.0 / D_MODEL)
            self.act(self.RINV[:, tt:tt + 1], self.ACC2[:, tt:tt + 1], AF.Exp, r=[("ACC2", tt)], w=[("RINV", tt)], scale=-0.5)
        for tt in range(4):
            i = self.out_cnt % 2
            self.out_cnt += 1
            self.stt(self.OUTB[i][:, :], self.X[:, tt, :], self.RINV[:, tt:tt + 1], self.GFIN[:, :], ALU.mult, ALU.mult,
                     r=[("X", tt), ("RINV", tt), "GFIN"], w=[("OUTB", i)])
            self.dma("pool", self.y[row0 + tt * 128:row0 + (tt + 1) * 128, :], self.OUTB[i][:, :], r=[("OUTB", i)], w=[],
                     slot=("out", i))
            if next_row0 is not None:
                self.xload(next_row0, tt)
                self.x_preloaded = True


def _colblock(W, c0, ncol):
    return np.ascontiguousarray(W[:, c0:c0 + ncol].reshape(8, 128, ncol).transpose(1, 0, 2)).reshape(128, 8 * ncol)


def pack_slabs(w_in, wg, wf, wo, w_up, w_down):
    sl = np.zeros((NSLAB, 128, 4096), np.float32)
    small = np.concatenate([w_in[:, 1536:1552], w_in[:, 3088:3096]], axis=1)
    sl[0][:, :192] = _colblock(small, 0, 24)
    sl[1] = _colblock(w_in, 0, 512)
    sl[2] = _colblock(w_in, 512, 512)
    sl[3] = _colblock(w_in, 1024, 512)
    sl[4] = _colblock(w_in, 1552, 512)
    sl[5] = _colblock(w_in, 2064, 512)
    sl[6] = _colblock(w_in, 2576, 512)
    wg3 = wg.reshape(4, 128, 1024).transpose(1, 0, 2)
    sl[7][:, :2048] = np.ascontiguousarray(wg3[:, :, 0:512]).reshape(128, 2048)
    sl[8] = _colblock(w_in, 3096, 512)
    sl[9][:, :2048] = np.ascontiguousarray(wg3[:, :, 512:1024]).reshape(128, 2048)
    sl[10] = _colblock(w_in, 3608, 512)
    wf3 = wf.reshape(4, 128, 1024).transpose(1, 0, 2)
    sl[11][:, :2048] = np.ascontiguousarray(wf3[:, :, 0:512]).reshape(128, 2048)
    sl[12] = _colblock(w_in, 4120, 512)
    sl[13][:, :2048] = np.ascontiguousarray(wf3[:, :, 512:1024]).reshape(128, 2048)
    sl[14] = _colblock(w_in, 4632, 512)
    sl[15] = _colblock(wo, 0, 512)
    sl[16] = _colblock(wo, 512, 512)
    for cb in range(6):
        ncol = 512 if cb < 5 else 256
        sl[17 + 2 * cb][:, :8 * ncol] = _colblock(w_up, cb * 512, ncol)
        sl[18 + 2 * cb][:, :8 * ncol] = _colblock(w_up, D_FF + cb * 512, ncol)
    for half in range(2):
        for g in range(3):
            nch = 8 if g < 2 else 6
            blk = w_down[g * 8 * 128:(g * 8 + nch) * 128, half * 512:(half + 1) * 512]
            sl[29 + half * 3 + g][:, :nch * 512] = np.ascontiguousarray(
                blk.reshape(nch, 128, 512).transpose(1, 0, 2)).reshape(128, nch * 512)
    return sl


def make_consts():
    c = np.zeros((128, CW), np.float32)
    c[:, 0:128] = np.eye(128, dtype=np.float32)
    m = np.arange(128)[:, None]
    l = np.arange(128)[None, :]
    c[:, 128:256] = (m <= l).astype(np.float32)
    c[:, 256:384] = np.where(m > l, -30000.0, 0.0)
    c[:, 384:512] = 1.0
    c[64, 512:576] = 1.0
    c[:, 576:1088] = 1.0
    return c


def pack_params(p):
    a = np.zeros((128, PW), np.float32)
    a[:, 0:8] = p["norm_mix_g"][0].reshape(8, 128).T
    a[:, 8:16] = p["norm_ffn_g"][0].reshape(8, 128).T
    a[:, 16:20] = p["gla_out_norm_g"][0].reshape(4, 128).T
    a[:, 20:28] = p["gate_b"][0, 0].reshape(8, 128).T
    a[:, 28:36] = p["gate_b"][0, 1].reshape(8, 128).T
    for j in range(3):
        a[:, 36 + j * NCC:36 + (j + 1) * NCC] = p["conv_w"][0, j].reshape(NCC, 128).T
    a[:, 102:124] = p["conv_b"][0].reshape(NCC, 128).T
    a[:, 124:126] = p["gla_alpha_b"][0].reshape(2, 128).T
    a[0:8, 126] = p["fox_forget_b"][0]
    a[0:64, 127] = p["fox_q_norm_g"][0]
    a[64:128, 127] = p["fox_q_norm_g"][0]
    a[0:64, 128] = p["fox_k_norm_g"][0]
    a[64:128, 128] = p["fox_k_norm_g"][0]
    return a


def run_cores(x, p, ncores, nseq, S, debug=False):
    f = lambda k: np.asarray(p[k], dtype=np.float32)
    pp = {k: f(k) for k in p if k != "x"}
    sl = pack_slabs(pp["w_in"][0], pp["w_gla_branch"][0], pp["w_fox_branch"][0], pp["w_out"][0],
                    pp["w_up"][0], pp["w_down"][0])
    cst = make_consts()
    par = pack_params(pp)
    w2 = np.ascontiguousarray(pp["gla_alpha_w2"][0])
    gfin = np.ascontiguousarray(pp["norm_final_g"])
    bld = Builder(nseq, S)
    bld.debug = debug
    nc = bld.build()
    in_maps = []
    for c in range(ncores):
        xc = np.ascontiguousarray(x[c * nseq:(c + 1) * nseq].reshape(nseq * S, D_MODEL))
        in_maps.append({"x": xc, "wsl": sl, "cst": cst, "par": par, "w2": w2, "gfin": gfin})
    res = run_bass_kernel_spmd(nc, in_maps, core_ids=list(range(ncores)))
    outs = [np.asarray(res.results[c]["y"]).reshape(nseq, S, D_MODEL) for c in range(ncores)]
    out = np.concatenate(outs, axis=0).astype(np.float32)
    if debug:
        return out, {k: np.asarray(v) for k, v in res.results[0].items() if k.startswith("dbg_")}
    return out


def kernel(**inputs):
    x = np.asarray(inputs["x"], dtype=np.float32)
    nseq = BATCH // NCORES
    return run_cores(x, inputs, NCORES, nseq, SEQ)
```
